# Optimizing a Trainium2 kernel written in Bass

```python
import jax, jax.numpy as jnp
from jax import lax
import numpy as np

D_MODEL = 1024
BATCH = 16
SEQ = 2048
DEPTH = 1

D_MIX = D_MODEL
D_LRU = D_MIX // 2
D_ATTN = D_MIX - D_LRU
LRU_BLOCKS = 8
LRU_BLOCK_W = D_LRU // LRU_BLOCKS
CONV_W = 4
LRU_C = 8.0
N_HEADS = 8
HEAD_DIM = D_ATTN // N_HEADS
MOBA_BLOCK = 256
MOBA_TOPK = 3
Q_CHUNK = 16
EPS = 1e-6
D_IN = 2 * D_LRU + 4 * D_ATTN

kernel_name = "hymba_rglru_moba_adaln_layer"


def rms_norm(x, g):
    xf = x.astype(jnp.float32)
    y = xf * lax.rsqrt(jnp.mean(xf * xf, axis=-1, keepdims=True) + EPS)
    return (y * g.astype(jnp.float32)).astype(x.dtype)


def causal_depthwise_conv(x, w, b):
    y = lax.conv_general_dilated(
        x, w[:, None, :].astype(x.dtype), window_strides=(1,),
        padding=[(CONV_W - 1, 0)], dimension_numbers=("NWC", "WIO", "NWC"),
        feature_group_count=x.shape[-1])
    return y + b


def rg_lru(xb, w_a, b_a, w_i, b_i, lam):
    B, S, _ = xb.shape
    xg = xb.reshape(B, S, LRU_BLOCKS, LRU_BLOCK_W)
    r = jax.nn.sigmoid(jnp.einsum("bsgi,gij->bsgj", xg, w_a).reshape(B, S, D_LRU) + b_a)
    i = jax.nn.sigmoid(jnp.einsum("bsgi,gij->bsgj", xg, w_i).reshape(B, S, D_LRU) + b_i)
    log_a = -LRU_C * r.astype(jnp.float32) * jax.nn.softplus(-lam.astype(jnp.float32))
    a = jnp.exp(log_a)
    mult = jnp.sqrt(-jnp.expm1(2.0 * log_a))
    first = (jnp.arange(S) == 0)[None, :, None]
    mult = jnp.where(first, 1.0, mult)
    bterm = mult * (i * xb).astype(jnp.float32)

    def combine(lhs, rhs):
        a1, b1 = lhs
        a2, b2 = rhs
        return a1 * a2, a2 * b1 + b2

    _, h = lax.associative_scan(combine, (a, bterm), axis=1)
    return h.astype(xb.dtype)


def moba_attention(q, k, v):
    B, S, H, Dh = q.shape
    s_pad = ((S + MOBA_BLOCK - 1) // MOBA_BLOCK) * MOBA_BLOCK
    pad = ((0, 0), (0, s_pad - S), (0, 0), (0, 0))
    q = jnp.pad(q, pad).transpose(0, 2, 1, 3)
    k = jnp.pad(k, pad).transpose(0, 2, 1, 3)
    v = jnp.pad(v, pad).transpose(0, 2, 1, 3)
    nb = s_pad // MOBA_BLOCK
    kb = k.reshape(B, H, nb, MOBA_BLOCK, Dh)
    vb = v.reshape(B, H, nb, MOBA_BLOCK, Dh)

    k_mean = jnp.mean(kb.astype(jnp.float32), axis=3)
    gate = jnp.einsum("bhsd,bhnd->bhsn", q.astype(jnp.float32), k_mean)
    q_blk = jnp.arange(s_pad) // MOBA_BLOCK
    past = jnp.arange(nb)[None, :] < q_blk[:, None]
    gate = jnp.where(past, gate, -jnp.inf)
    k_sel = max(1, min(MOBA_TOPK, nb - 1))
    top_val, top_idx = lax.top_k(gate, k_sel)
    valid = jnp.isfinite(top_val)

    n_chunks = s_pad // Q_CHUNK
    q_c = jnp.moveaxis(q.reshape(B, H, n_chunks, Q_CHUNK, Dh), 2, 0)
    idx_c = jnp.moveaxis(top_idx.reshape(B, H, n_chunks, Q_CHUNK, k_sel), 2, 0)
    ok_c = jnp.moveaxis(valid.reshape(B, H, n_chunks, Q_CHUNK, k_sel), 2, 0)
    bi = jnp.arange(B)[:, None, None, None]
    hi = jnp.arange(H)[None, :, None, None]
    scale = HEAD_DIM ** -0.5

    def chunk_fn(args):
        ci, qc, idx, ok = args
        k_past = kb[bi, hi, idx]
        v_past = vb[bi, hi, idx]
        own = (ci * Q_CHUNK) // MOBA_BLOCK
        k_own = lax.dynamic_index_in_dim(kb, own, axis=2, keepdims=False)
        v_own = lax.dynamic_index_in_dim(vb, own, axis=2, keepdims=False)
        s_past = jnp.einsum("bhqd,bhqnkd->bhqnk", qc, k_past).astype(jnp.float32) * scale
        s_past = jnp.where(ok[..., None], s_past, -jnp.inf).reshape(B, H, Q_CHUNK, k_sel * MOBA_BLOCK)
        s_own = jnp.einsum("bhqd,bhkd->bhqk", qc, k_own).astype(jnp.float32) * scale
        q_pos = ci * Q_CHUNK + jnp.arange(Q_CHUNK)
        k_pos = own * MOBA_BLOCK + jnp.arange(MOBA_BLOCK)
        s_own = jnp.where(k_pos[None, :] <= q_pos[:, None], s_own, -jnp.inf)
        p = jax.nn.softmax(jnp.concatenate([s_past, s_own], axis=-1), axis=-1).astype(v.dtype)
        p_past = p[..., : k_sel * MOBA_BLOCK].reshape(B, H, Q_CHUNK, k_sel, MOBA_BLOCK)
        p_own = p[..., k_sel * MOBA_BLOCK:]
        return (jnp.einsum("bhqnk,bhqnkd->bhqd", p_past, v_past)
                + jnp.einsum("bhqk,bhkd->bhqd", p_own, v_own))

    out = lax.map(chunk_fn, (jnp.arange(n_chunks), q_c, idx_c, ok_c))
    out = jnp.moveaxis(out, 0, 2).reshape(B, H, s_pad, Dh).transpose(0, 2, 1, 3)
    return out[:, :S]


def hybrid_layer(x, c, w_ada, b_ada, norm_g, w_in, conv_w, conv_b,
                 lru_wa, lru_ba, lru_wi, lru_bi, lru_lambda, q_norm_g, k_norm_g, w_out):
    B, S, _ = x.shape
    ada = jax.nn.silu(c) @ w_ada + b_ada
    shift, scale, gate = jnp.split(ada, 3, axis=-1)
    h = rms_norm(x, norm_g) * (1.0 + scale[:, None, :]) + shift[:, None, :]
    z = h @ w_in
    cuts = [D_LRU, 2 * D_LRU, 2 * D_LRU + D_ATTN, 2 * D_LRU + 2 * D_ATTN, 2 * D_LRU + 3 * D_ATTN]
    xl, gl, q, k, v, ga = jnp.split(z, cuts, axis=-1)

    xl = causal_depthwise_conv(xl, conv_w, conv_b)
    y_lru = rg_lru(xl, lru_wa, lru_ba, lru_wi, lru_bi, lru_lambda) * jax.nn.silu(gl)

    q = rms_norm(q.reshape(B, S, N_HEADS, HEAD_DIM), q_norm_g)
    k = rms_norm(k.reshape(B, S, N_HEADS, HEAD_DIM), k_norm_g)
    v = v.reshape(B, S, N_HEADS, HEAD_DIM)
    y_att = moba_attention(q, k, v).reshape(B, S, D_ATTN) * jax.nn.silu(ga)

    y = jnp.concatenate([y_lru, y_att], axis=-1) @ w_out
    return x + gate[:, None, :] * y


def setup_inputs(seed: int = 0) -> dict:
    key = jax.random.key(seed)
    ks = jax.random.split(key, 16)
    f32 = jnp.float32
    x = jax.random.normal(ks[0], (BATCH, SEQ, D_MODEL), f32)
    c = jax.random.normal(ks[1], (BATCH, D_MODEL), f32)
    w_ada = jax.random.normal(ks[2], (DEPTH, D_MODEL, 3 * D_MODEL), f32) * D_MODEL ** -0.5
    b_ada = jax.random.normal(ks[3], (DEPTH, 3 * D_MODEL), f32) * 0.01
    norm_g = 1.0 + 0.02 * jax.random.normal(ks[4], (DEPTH, D_MODEL), f32)
    w_in = jax.random.normal(ks[5], (DEPTH, D_MODEL, D_IN), f32) * D_MODEL ** -0.5
    conv_w = jax.random.normal(ks[6], (DEPTH, CONV_W, D_LRU), f32) * CONV_W ** -0.5
    conv_b = jax.random.normal(ks[7], (DEPTH, D_LRU), f32) * 0.01
    lru_wa = jax.random.normal(ks[8], (DEPTH, LRU_BLOCKS, LRU_BLOCK_W, LRU_BLOCK_W), f32) * LRU_BLOCK_W ** -0.5
    lru_ba = jax.random.normal(ks[9], (DEPTH, D_LRU), f32) * 0.01
    lru_wi = jax.random.normal(ks[10], (DEPTH, LRU_BLOCKS, LRU_BLOCK_W, LRU_BLOCK_W), f32) * LRU_BLOCK_W ** -0.5
    lru_bi = jax.random.normal(ks[11], (DEPTH, D_LRU), f32) * 0.01
    a0 = jax.random.uniform(ks[12], (DEPTH, D_LRU), f32, minval=0.9, maxval=0.999)
    lru_lambda = jnp.log(a0) - jnp.log1p(-a0)
    q_norm_g = 1.0 + 0.02 * jax.random.normal(ks[13], (DEPTH, HEAD_DIM), f32)
    k_norm_g = 1.0 + 0.02 * jax.random.normal(ks[14], (DEPTH, HEAD_DIM), f32)
    w_out = jax.random.normal(ks[15], (DEPTH, D_MIX, D_MODEL), f32) * D_MIX ** -0.5
    return {"x": x, "c": c, "w_ada": w_ada, "b_ada": b_ada, "norm_g": norm_g, "w_in": w_in,
            "conv_w": conv_w, "conv_b": conv_b, "lru_wa": lru_wa, "lru_ba": lru_ba,
            "lru_wi": lru_wi, "lru_bi": lru_bi, "lru_lambda": lru_lambda,
            "q_norm_g": q_norm_g, "k_norm_g": k_norm_g, "w_out": w_out}


def reference(x, c, w_ada, b_ada, norm_g, w_in, conv_w, conv_b, lru_wa, lru_ba,
              lru_wi, lru_bi, lru_lambda, q_norm_g, k_norm_g, w_out):
    for l in range(DEPTH):
        x = hybrid_layer(x, c, w_ada[l], b_ada[l], norm_g[l], w_in[l], conv_w[l], conv_b[l],
                         lru_wa[l], lru_ba[l], lru_wi[l], lru_bi[l], lru_lambda[l],
                         q_norm_g[l], k_norm_g[l], w_out[l])
    return x
```

```python
import contextlib
import numpy as np
import concourse.bass as bass
import concourse.mybir as mybir
from concourse.bass_utils import run_bass_kernel_spmd

F32 = mybir.dt.float32
BF16 = mybir.dt.bfloat16
AF = mybir.ActivationFunctionType
ALU = mybir.AluOpType
AX = mybir.AxisListType

NCORES = 8
SEQ = 2048
D = 1024
NSEQ = 2
EPS = 1e-6
NEG = -1.0e30

C_IDENT, C_TRI, C_ESEL, C_ONES4, C_DCOR, C_ONES64, C_IND2, C_IND4, C_NBM, NCON = (
    0, 128, 256, 1280, 1336, 1464, 1528, 1530, 1658, 2170)
NCB = 1658
C_MASKN = 1786
P_CT, P_BADA, P_NORMG, P_CONVW, P_CONVB, P_LBA, P_LBI, P_LAM, P_GQ, P_GK, NPAR = (
    0, 16, 40, 48, 64, 68, 72, 76, 80, 81, 82)


def _consts():
    c = np.zeros((128, NCON), np.float32)
    c[:, C_IDENT:C_IDENT + 128] = np.eye(128, dtype=np.float32)
    k = np.arange(128)[:, None]
    q = np.arange(128)[None, :]
    c[:, C_TRI:C_TRI + 128] = (k <= q).astype(np.float32)
    mk = np.where(k > q, -30000.0, 0.0).astype(np.float32)
    c[:, C_MASKN:C_MASKN + 128] = mk
    isw = np.zeros((128, 128), np.float32)
    isw[np.arange(128), (np.arange(128) + 64) % 128] = 1.0
    c[:, C_MASKN + 128:C_MASKN + 256] = isw
    c[:, C_MASKN + 256:C_MASKN + 384] = np.concatenate([mk[64:128], mk[0:64]], axis=0)
    for i in range(8):
        for hl in range(2):
            c[hl * 8 + i, C_ESEL + i * 128 + hl * 64: C_ESEL + i * 128 + hl * 64 + 64] = 1.0
    for col in (7, 15, 39, 47):
        c[:, C_ONES4 + col] = 1.0
    for r in range(16):
        hl = r // 8
        c[r, C_DCOR + hl * 64: C_DCOR + hl * 64 + 64] = -256.0
        c[r, C_NBM + hl * 64: C_NBM + hl * 64 + 64] = -1.0
        c[32 + r, C_NBM + hl * 64: C_NBM + hl * 64 + 64] = -1.0
    c[:, C_ONES64:C_ONES64 + 64] = 1.0
    c[0:64, C_IND2] = 1.0
    c[64:128, C_IND2 + 1] = 1.0
    for r in range(4):
        c[r, C_IND4 + (r % 2) * 64: C_IND4 + (r % 2) * 64 + 64] = 1.0
    return c


class Prog:
    def __init__(self):
        self.ops = []

    DEFC = {"pe": 0.25, "act": 0.65, "dve": 0.65, "pool": 1.0, "sp": 0.1}

    def op(self, eng, fn, r=(), w=(), dma=None, c=None, tb=None, md=None, nb=524288):
        rk, wk = [], []
        for b in r:
            rk.extend(b if isinstance(b, list) else [b])
        for b in w:
            wk.extend(b if isinstance(b, list) else [b])
        if c is None:
            c = self.DEFC[eng]
        if eng == "pe":
            tb = md or "f"
        if eng == "act" and tb is None:
            names = fn.__code__.co_names
            if "Exp" in names or "Tanh" in names:
                tb = "e"
            elif "Sqrt" in names:
                tb = "s"
            elif "Ln" in names:
                tb = "l"
        self.ops.append(dict(eng=eng, fn=fn, r=rk, w=wk, dma=dma, c=c, tb=tb, ph=getattr(self, 'ph', ''), nb=nb))

    def schedule(self):
        ops = self.ops
        n = len(ops)
        succ = [[] for _ in range(n)]
        indeg = [0] * n
        for i, o in enumerate(ops):
            for d in o["deps"]:
                succ[d].append(i)
                indeg[i] += 1
        LAT = getattr(self, "LAT", 0.35)
        prio = [0.0] * n
        for i in range(n - 1, -1, -1):
            m = 0.0
            for s_ in succ[i]:
                if prio[s_] > m:
                    m = prio[s_]
            prio[i] = ops[i]["c"] + (3.0 if ops[i]["dma"] is not None else 0.0) + m
        for i in range(n):
            if ops[i]["dma"] is not None and not ops[i]["deps"]:
                prio[i] += 1e6 - i
        rt = [0.0] * n
        ready = {}
        efree = {}
        table = [None]
        pmode = ["f"]
        dma_free = [0.0]
        for i in range(n):
            efree.setdefault(ops[i]["eng"], 0.0)
            if indeg[i] == 0:
                ready.setdefault(ops[i]["eng"], []).append(i)
        order = []
        done = 0
        while done < n:
            best = None
            for eng, lst in ready.items():
                if not lst:
                    continue
                t_e = efree[eng]
                avail = [i for i in lst if rt[i] <= t_e]
                if avail:
                    if eng == "act":
                        same = [i for i in avail if ops[i]["tb"] is None or ops[i]["tb"] == table[0]]
                        if same:
                            avail = same
                    elif eng == "pe":
                        same = [i for i in avail if ops[i]["tb"] == pmode[0]]
                        if same:
                            avail = same
                    pick = max(avail, key=lambda i: (prio[i], -i)) if getattr(self, 'PRIO', 'cp') == 'cp' else min(avail)
                    st = t_e
                else:
                    pick = min(lst, key=lambda i: (rt[i], -prio[i]))
                    st = rt[pick]
                if best is None or st < best[0]:
                    best = (st, eng, pick)
            st, eng, i = best
            ready[eng].remove(i)
            o = ops[i]
            dur = o["c"]
            if eng == "act" and o["tb"] is not None and o["tb"] != table[0]:
                dur += 2.6
                table[0] = o["tb"]
            if eng == "pe" and o["tb"] != pmode[0]:
                dur += 0.12
                pmode[0] = o["tb"]
            fin = st + dur
            efree[eng] = fin
            o['t0'] = st
            o['t1'] = fin
            if o["dma"] is not None:
                dstart = max(fin + 1.5, dma_free[0])
                vis = dstart + o["nb"] / 250e3
                dma_free[0] = vis
            else:
                vis = fin
            order.append(i)
            done += 1
            for s_ in succ[i]:
                lat = 0.0 if (ops[s_]["eng"] == "pe" and eng == "pe" and o["dma"] is None) else LAT
                if vis + lat > rt[s_]:
                    rt[s_] = vis + lat
                indeg[s_] -= 1
                if indeg[s_] == 0:
                    ready.setdefault(ops[s_]["eng"], []).append(s_)
        remap = {old: new for new, old in enumerate(order)}
        newops = [ops[i] for i in order]
        for o in newops:
            o["deps"] = {remap[d] for d in o["deps"]}
        self.ops = newops
        self.sim_time = max(efree.values())

    def finalize(self):
        ops = self.ops
        last_w, readers = {}, {}
        for i, o in enumerate(ops):
            deps = set()
            for k in o["r"]:
                if k in last_w:
                    deps.add(last_w[k])
            for k in o["w"]:
                if k in last_w:
                    deps.add(last_w[k])
                for rr in readers.get(k, ()):
                    deps.add(rr)
            deps.discard(i)
            o["deps"] = deps
            for k in o["w"]:
                last_w[k] = i
                readers[k] = []
            for k in o["r"]:
                if k not in o["w"]:
                    readers.setdefault(k, []).append(i)
        if getattr(self, "do_sched", True):
            self.schedule()
            ops = self.ops
        needs = [False] * len(ops)
        for i, o in enumerate(ops):
            for d in o["deps"]:
                dd = ops[d]
                if dd["dma"] is None and not (dd["eng"] == "pe" and o["eng"] == "pe" and o["dma"] is None):
                    needs[d] = True
        cnt, dcnt = {}, {}
        for i, o in enumerate(ops):
            if o["dma"] is not None:
                dcnt[o["dma"]] = dcnt.get(o["dma"], 0) + 16
                o["sem"] = ("dma", o["dma"])
                o["val"] = dcnt[o["dma"]]
                o["inc"] = 16
            elif needs[i]:
                cnt[o["eng"]] = cnt.get(o["eng"], 0) + 1
                o["sem"] = ("eng", o["eng"])
                o["val"] = cnt[o["eng"]]
                o["inc"] = 1
            else:
                o["sem"] = None
        waited = {}
        for i, o in enumerate(ops):
            need = {}
            for d in o["deps"]:
                dd = ops[d]
                if dd["sem"] is None:
                    continue
                if dd["dma"] is None and dd["eng"] == "pe" and o["eng"] == "pe" and o["dma"] is None:
                    continue
                need[dd["sem"]] = max(need.get(dd["sem"], 0), dd["val"])
            wl = []
            for sk, v in need.items():
                key = (o["eng"], sk)
                if waited.get(key, 0) >= v:
                    continue
                waited[key] = v
                wl.append((sk, v))
            o["waits"] = wl
        self.final_dma = dict(dcnt)
        self.semkeys = sorted({o["sem"] for o in ops if o["sem"] is not None}, key=str)


def build_program(debug=False, nseq_run=NSEQ, do_b=True, NLS=2, OVW=14656, TILE_ORDER=('q', 'l0', 'v', 'k', 'l1', 'g')):
    nc = bass.Bass("TRN2", target_bir_lowering=False)
    ntok = NSEQ * SEQ
    x_d = nc.dram_tensor("x", [ntok, D], F32, kind="ExternalInput").ap()
    par_d = nc.dram_tensor("par", [128, NPAR], F32, kind="ExternalInput").ap()
    con_d = nc.dram_tensor("con", [128, NCON], F32, kind="ExternalInput").ap()
    wada_d = nc.dram_tensor("w_ada", [D, 3 * D], F32, kind="ExternalInput").ap()
    win_d = nc.dram_tensor("w_in", [D, 3 * D], F32, kind="ExternalInput").ap()
    wbd_d = nc.dram_tensor("wbd", [128, 1024], F32, kind="ExternalInput").ap()
    wout_d = nc.dram_tensor("w_out", [D, D], F32, kind="ExternalInput").ap()
    dwc_d = nc.dram_tensor("dwc", [128, 2048], F32, kind="ExternalInput").ap()
    out_d = nc.dram_tensor("out", [ntok, D], F32, kind="ExternalOutput").ap()

    P = Prog()
    es = contextlib.ExitStack()

    def sb(name, shape, dt):
        return es.enter_context(nc.sbuf_tensor(name, shape, dt))

    WI = sb("WI", [128, 8, 3072], BF16)
    QKG = sb("QKG", [128, 12, 2048], BF16)
    VT = sb("VT", [128, 16, 512], BF16)
    YL = sb("YL", [128, 4, 2048], BF16)
    IDF = sb("IDF", [128, 128], F32)
    ONESF = sb("ONESF", [128, 128], F32)
    WBD = sb("WBD", [128, 8, 128], BF16)
    DW = sb("DW", [128, 16, 128], BF16)
    PAR = sb("PAR", [128, NPAR], F32)
    SM = sb("SM", [128, 256], F32)
    SMB = sb("SMB", [128, 64], BF16)
    CB = sb("CB", [128, NCB], BF16)
    NBM = sb("NBM", [128, 128], F32)
    MASKN = sb("MASKN", [128, 3, 128], BF16)
    XT = sb("XT", [128, 2, 1024], F32)
    NVS = sb("NVS", [128, 4, 128], BF16)
    KMX = sb("KMX", [128, 4, 16], BF16)
    OV = sb("OV", [128, OVW], F32)
    PSALL = es.enter_context(nc.psum_tensor("PSALL", [128, 4096], F32))
    PS = [PSALL[:, i * 512:(i + 1) * 512] for i in range(8)]

    WADA = QKG[:].rearrange("p a b -> p (a b)").rearrange("p (k n) -> p k n", k=8)
    YLX = YL[:].rearrange("p a b -> p (a b)").bitcast(F32)

    def ovf(off, n):
        return OV[:, off:off + n], [("ov", pg) for pg in range(off // 64, (off + n + 63) // 64)]

    def ovb(off, n):
        assert n % 2 == 0
        return OV[:, off:off + n // 2].bitcast(BF16), [("ov", pg) for pg in range(off // 64, (off + n // 2 + 63) // 64)]

    o = 0
    HN = []
    for i in range(2):
        HN.append(ovb(o, 1024)); o += 512
    HT2 = []
    for i in range(2):
        a_, k_ = ovb(o, 8 * 512); o += 2048
        HT2.append((a_.rearrange("p (k n) -> p k n", k=8), k_))
    XL = []
    for i in range(4):
        XL.append(ovb(o, 520)); o += 260
    LS = []
    lru_base = o
    for i in range(NLS):
        d_ = {}
        for nm in ("XC", "AA", "MM", "BB", "HH"):
            d_[nm] = ovf(o, 512); o += 512
        d_["GS"] = ovb(o, 512); o += 256
        d_["XCB"] = ovb(o, 512); o += 256
        LS.append(d_)
    SQ = []
    QC = []
    for i in range(2):
        SQ.append(ovb(o, 512)); o += 256
        QC.append(ovf(o, 512)); o += 512
    TGA = ovf(o, 512); o += 512
    RSXH = ovb(o, 512); o += 256
    assert o <= OVW, o
    CST = ovf(lru_base, NCON)
    so = lru_base
    assert so <= OVW
    o = 0
    GSS = ovf(o, 512); o += 512
    STMP = []
    for i in range(6):
        STMP.append(ovf(o, 128)); o += 128
    SELa, SEL_k = ovb(o, 8 * 64); o += 256
    SEL = SELa.rearrange("p (q h c) -> p q h c", q=8, h=4)
    UNSa, UNS_k = ovb(o, 8 * 4 * 48); o += 768
    UNS = UNSa.rearrange("p (q h c) -> p q h c", q=8, h=4)
    VST1 = ovf(o, 512); o += 512
    VSHI = ovb(o, 512); o += 256
    dg_off = o
    DG_ap, DG_k = ovf(o, 1024); o += 1024
    DG = DG_ap.rearrange("p (f n) -> p f n", f=8)
    SELT2 = []
    UNST2 = []
    for i in range(2):
        SELT2.append(ovb(o, 256)); o += 128
        UNST2.append(ovb(o, 256)); o += 128
    od0 = ovf(o, 512); o += 512
    assert o <= 5120, o
    o = 5120
    WO_ap, WO_k = ovb(o, 8192); o += 4096
    WO = WO_ap.rearrange("p (k n) -> p k n", k=8)
    QP2 = []
    for i in range(2):
        a_, k_ = ovb(o, 8 * 256); o += 1024
        QP2.append((a_.rearrange("p (i n) -> p i n", i=8), k_))
    PTA = []
    PTBb = []
    PTAB = []
    for i in range(2):
        PTAB.append(OV[:, o:o + 512].bitcast(BF16))
        PTA.append(ovb(o, 512)); o += 256
        PTBb.append(ovb(o, 512)); o += 256
    RD2 = []
    T22 = []
    for i in range(2):
        RD2.append(ovf(o, 256)); o += 256
        T22.append(ovf(o, 256)); o += 256
    YA = []
    for i in range(2):
        a_, k_ = ovb(o, 1024); o += 512
        YA.append((a_.rearrange("p (h n) -> p h n", h=4), k_))
    OD2 = [od0, ovf(dg_off, 512)]
    assert o <= OVW, o

    def sm(a, b=None):
        return SM[:, a:(a + 1 if b is None else b)]
    S_TC, S_ADAF, S_AMOD, S_HC, S_HC2, S_HBA, S_HBI, S_QCOL, S_KCOL, S_ONE, S_NHALF = 0, 192, 48, 64, 68, 72, 76, 80, 81, 82, 83
    S_EX, S_SP, S_SSQ, S_VV, S_RSTD, S_HST, S_KMS, S_SSQS, S_RSC, S_ZERO = 84, 88, 96, 100, 104, 108, 112, 144, 160, 176
    SCB = SMB[:, 0:16]
    RHL = SMB[:, 16:32]

    def cb(off, n, rows=128):
        return CB[0:rows, off:off + n]
    IDENT = cb(C_IDENT, 128)
    TRI = cb(C_TRI, 128)
    ONES64 = cb(C_ONES64, 64)
    IND2 = cb(C_IND2, 2)
    IND4 = cb(C_IND4, 128, 4)
    DCOR = cb(C_DCOR, 128, 48)

    def psb(i):
        return PS[i][:].bitcast(BF16)

    P.op("sp", lambda e: e.dma_start(out=PAR[:], in_=par_d[:, :]), w=["PAR"], dma="par", nb=65536)
    P.op("sp", lambda e: e.dma_start(out=CST[0], in_=con_d[:, :]), w=[CST[1]], dma="cst")
    qkg_keys = [("QKG", i) for i in range(12)]
    vt_keys = [("VT", i) for i in range(16)]
    VTW = VT[:].rearrange("p a b -> p (a b)").rearrange("p (k n) -> p k n", k=8)
    for kt in range(8):
        P.op("pool", lambda e, kt=kt: e.dma_start(out=WADA[:, kt, 0:2048], in_=wada_d[kt * 128:(kt + 1) * 128, 0:2048]),
             w=[("WADA", kt)] + (qkg_keys if kt == 0 else []), dma="wada%d" % kt, nb=1 << 20)
    win_v = win_d.rearrange("(k p) n -> p k n", p=128)
    for cbk in (2, 0, 1, 3, 5, 4):
        P.op("pool", lambda e, cbk=cbk: e.dma_start(out=WI[:, :, cbk * 512:(cbk + 1) * 512], in_=win_v[:, :, cbk * 512:(cbk + 1) * 512]),
             w=[("WI", cbk)], dma="wi%d" % cbk, nb=2 << 20)
        if cbk == 2:
            P.op("pool", lambda e: e.dma_start(out=WBD[:].rearrange("p a b -> p (a b)"), in_=wbd_d[:, :]), w=["WBD"], dma="wbd")
            P.op("pool", lambda e: e.dma_start(out=DW[:].rearrange("p a b -> p (a b)"), in_=dwc_d[:, :]), w=["DW"], dma="dwc", nb=1 << 20)
    for kt in range(8):
        P.op("pool", lambda e, kt=kt: e.dma_start(out=VTW[:, kt, :], in_=wada_d[kt * 128:(kt + 1) * 128, 2048:3072]),
             w=[("WADAg", kt)] + (vt_keys if kt == 0 else []), dma="wadag%d" % kt)
    P.op("dve", lambda e: e.tensor_copy(CB[:], CST[0][:, 0:NCB]), r=[CST[1]], w=["CB"])
    P.op("dve", lambda e: e.tensor_copy(NBM[:], CST[0][:, C_NBM:C_NBM + 128]), r=[CST[1]], w=["NBM"])
    P.op("dve", lambda e: e.tensor_copy(MASKN[:].rearrange("p a b -> p (a b)"), CST[0][:, C_MASKN:C_MASKN + 384]), r=[CST[1]], w=["CB"])
    P.op("dve", lambda e: e.tensor_copy(IDF[:], CST[0][:, C_IDENT:C_IDENT + 128]), r=[CST[1]], w=["IDF"])
    P.op("dve", lambda e: e.memset(ONESF[:], 1.0), w=["IDF"])
    P.op("dve", lambda e: e.memset(sm(S_ONE), 1.0), w=["SMc"])
    P.op("dve", lambda e: e.memset(sm(S_NHALF), -0.5), w=["SMc"])
    P.op("dve", lambda e: e.memset(sm(S_ZERO), 0.0), w=["SMc"])
    P.op("dve", lambda e: e.memset(NVS[:], 0.0), w=["NVS"])
    P.op("dve", lambda e: e.memset(KMX[:], 0.0), w=["KMX"])
    P.op("act", lambda e: e.activation(sm(S_EX, S_EX + 4), PAR[:, P_LAM:P_LAM + 4], AF.Exp, scale=-1.0), r=["PAR"], w=["EX"])
    P.op("act", lambda e: e.activation(sm(S_SP, S_SP + 4), sm(S_EX, S_EX + 4), AF.Ln, bias=sm(S_ONE)), r=["EX", "SMc"], w=["SP"])
    P.op("dve", lambda e: e.tensor_scalar(sm(S_HC, S_HC + 4), sm(S_SP, S_SP + 4), -4.0, None, ALU.mult), r=["SP"], w=["HC"])
    P.op("dve", lambda e: e.tensor_scalar(sm(S_HC2, S_HC2 + 4), sm(S_SP, S_SP + 4), -8.0, None, ALU.mult), r=["SP"], w=["HC"])
    P.op("dve", lambda e: e.tensor_scalar(sm(S_HBA, S_HBA + 4), PAR[:, P_LBA:P_LBA + 4], 0.5, None, ALU.mult), r=["PAR"], w=["HC"])
    P.op("dve", lambda e: e.tensor_scalar(sm(S_HBI, S_HBI + 4), PAR[:, P_LBI:P_LBI + 4], 0.5, None, ALU.mult), r=["PAR"], w=["HC"])
    P.op("dve", lambda e: e.scalar_tensor_tensor(sm(S_QCOL), PAR[:, P_GQ:P_GQ + 1], 0.125, PAR[:, P_GK:P_GK + 1], ALU.mult, ALU.mult),
         r=["PAR"], w=["HC"])
    P.op("act", lambda e: e.activation(sm(S_TC, S_TC + 16), PAR[:, P_CT:P_CT + 16], AF.Tanh, scale=0.5), r=["PAR"], w=["TC"])
    P.op("dve", lambda e: e.scalar_tensor_tensor(SCB, sm(S_TC, S_TC + 16), 1.0, PAR[:, P_CT:P_CT + 16], ALU.add, ALU.mult),
         r=["TC", "PAR"], w=["SCB"])
    ADP = PS[0]
    for ft in range(24):
        for kt in range(8):
            if ft < 16:
                P.op("pe", lambda e, ft=ft, kt=kt: e.matmul(ADP[:, ft * 2:ft * 2 + 2], WADA[:, kt, ft * 128:(ft + 1) * 128],
                                                          SMB[:, kt:kt + 9:8], start=(kt == 0), stop=(kt == 7)),
                     r=[("WADA", kt), ("WADA", 7), "SCB"] + qkg_keys, w=[("ps", 0)], c=0.08)
            else:
                P.op("pe", lambda e, ft=ft, kt=kt: e.matmul(PS[1][:, (ft - 16) * 2:(ft - 16) * 2 + 2], VTW[:, kt, (ft - 16) * 128:(ft - 15) * 128],
                                                          SMB[:, kt:kt + 9:8], start=(kt == 0), stop=(kt == 7)),
                     r=[("WADAg", kt), ("WADAg", 7), "SCB"] + vt_keys, w=[("ps", 1)], c=0.08)
    ADAF = SM[:, S_ADAF:S_ADAF + 48].rearrange("p (f b) -> p f b", b=2)
    bada_g = bass.AP(PAR[:].tensor, PAR[:, P_BADA + 16:P_BADA + 24].offset,
                     [list(PAR[:, P_BADA + 16:P_BADA + 24].ap[0]), [1, 8], [0, 2]])
    bada_b = bass.AP(PAR[:].tensor, PAR[:, P_BADA:P_BADA + 16].offset,
                     [list(PAR[:, P_BADA:P_BADA + 16].ap[0]), [1, 16], [0, 2]])
    P.op("dve", lambda e: e.scalar_tensor_tensor(ADAF[:, 16:24, :], PS[1][:, 0:16].rearrange("p (f b) -> p f b", b=2), 0.5, bada_g, ALU.mult, ALU.add),
         r=[("ps", 1), "PAR"], w=["ADAFg"])
    P.op("dve", lambda e: e.scalar_tensor_tensor(ADAF[:, 0:16, :], ADP[:, 0:32].rearrange("p (f b) -> p f b", b=2), 0.5, bada_b, ALU.mult, ALU.add),
         r=[("ps", 0), "PAR"], w=["ADAF"])
    ng_b = bass.AP(PAR[:].tensor, PAR[:, P_NORMG:P_NORMG + 8].offset, [list(PAR[:, P_NORMG:P_NORMG + 8].ap[0]), [1, 8], [0, 2]])
    AMOD = SM[:, S_AMOD:S_AMOD + 16].rearrange("p (f b) -> p f b", b=2)
    P.op("dve", lambda e: e.scalar_tensor_tensor(AMOD, ADAF[:, 8:16, :], 1.0, ng_b, ALU.add, ALU.mult), r=["ADAF", "PAR"], w=["AMOD"])

    def qk_key(i):
        return ("QKG", i)

    for s in range(nseq_run):
        for T in range(4):
            P.ph = 'A%d.%d' % (s, T)
            HT, HT_k = HT2[T % 2]
            t0 = s * SEQ + T * 512
            l0 = T * 512
            for u in range(4):
                xs = u % 2
                tok = t0 + u * 128
                ptb = [3, 0, 1, 2][u] if (s == 0 and T == 0) else 3
                if s == 0 and T == 0:
                    xsrc = YLX[:, u * 1024:(u + 1) * 1024]
                    xk = [("YL", c_) for c_ in range(4)] + [("YLX", u)]
                    P.op("sp", lambda e, xsrc=xsrc, tok=tok: e.dma_start(out=xsrc, in_=x_d[tok:tok + 128, :]), w=[("YLX", u)], dma="xy%d" % u)
                else:
                    xsrc = XT[:, xs, :]
                    xk = [("XT", xs)]
                    P.op("sp", lambda e, xs=xs, tok=tok: e.dma_start(out=XT[:, xs, :], in_=x_d[tok:tok + 128, :]), w=xk, dma="xt%d" % xs)
                P.op("act", lambda e, xs=xs, u=u, xsrc=xsrc: e.activation(HN[xs][0], xsrc, AF.Square, accum_out=sm(S_SSQ + u)),
                     r=xk, w=[HN[xs][1], ("SSQ", u)], c=1.1)
                P.op("dve", lambda e, u=u: e.tensor_scalar(sm(S_VV + u), sm(S_SSQ + u), 1.0 / D, EPS, ALU.mult, ALU.add),
                     r=[("SSQ", u)], w=[("VV", u)], c=0.15)
                P.op("pool", lambda e, u=u: e.tensor_tensor(sm(S_RSTD + u), sm(S_VV + u), sm(S_NHALF), ALU.pow),
                     r=[("VV", u), "SMc"], w=[("RSTD", u)], c=1.3)
                P.op("dve", lambda e, xs=xs, u=u, xsrc=xsrc: e.tensor_scalar(HN[xs][0], xsrc, sm(S_RSTD + u), None, ALU.mult),
                     r=xk + [("RSTD", u)], w=[HN[xs][1]], c=0.65)
                for kt in range(8):
                    P.op("pe", lambda e, xs=xs, kt=kt, ptb=ptb: e.transpose(psb(ptb)[:, kt * 128:(kt + 1) * 128], HN[xs][0][:, kt * 128:(kt + 1) * 128], IDENT),
                         r=[HN[xs][1], "CB"], w=[("ps", ptb)], c=0.08)
                for kt in range(8):
                    src = psb(ptb)[:, kt * 128:(kt + 1) * 128]
                    dst = HT[:, kt, u * 128:(u + 1) * 128]
                    if kt % 4 != 3:
                        P.op("dve", lambda e, src=src, dst=dst, kt=kt, s=s: e.tensor_scalar(dst, src, AMOD[:, kt, s:s + 1], ADAF[:, kt, s:s + 1], ALU.mult, ALU.add),
                             r=[("ps", ptb), "AMOD", "ADAF"], w=[HT_k], c=0.3)
                    else:
                        P.op("act", lambda e, src=src, dst=dst, kt=kt, s=s: e.activation(dst, src, AF.Identity, bias=ADAF[:, kt, s:s + 1], scale=AMOD[:, kt, s:s + 1]),
                             r=[("ps", ptb), "AMOD", "ADAF"], w=[HT_k], c=0.3)

            zrot = [0]

            def win_fm(ft):
                bank = (0, 1, 2, 6)[zrot[0] % 4]
                zrot[0] += 1
                for kt in range(8):
                    P.op("pe", lambda e, ft=ft, kt=kt, bank=bank, HT=HT: e.matmul(PS[bank][:, :], WI[:, kt, ft * 128:(ft + 1) * 128], HT[:, kt, :],
                                                                       start=(kt == 0), stop=(kt == 7)),
                         r=[("WI", ft // 4), HT_k], w=[("ps", bank)])
                return bank

            def emit_lru(cp):
                for cc in range(2):
                    ct = cp * 2 + cc
                    L = LS[(cp * 2 + cc) % NLS]
                    bank = win_fm(ct)
                    P.op("act", lambda e, ct=ct, bank=bank: e.activation(XL[ct][0][:, 3:515], PS[bank][:, :], AF.Copy),
                         r=[("ps", bank)], w=[XL[ct][1]])
                    if T == 0:
                        P.op("dve", lambda e, ct=ct: e.memset(XL[ct][0][:, 0:3], 0.0), w=[XL[ct][1]])
                    bank = win_fm(4 + ct)
                    P.op("act", lambda e, L=L, bank=bank: e.activation(L["HH"][0], PS[bank][:, :], AF.Tanh, scale=0.5),
                         r=[("ps", bank)], w=[L["HH"][1]])
                    P.op("dve", lambda e, L=L, bank=bank: e.scalar_tensor_tensor(L["GS"][0], L["HH"][0], 1.0, PS[bank][:, :], ALU.add, ALU.mult),
                         r=[("ps", bank), L["HH"][1]], w=[L["GS"][1]])
                    for jj in range(4):
                        P.op("pe", lambda e, ct=ct, jj=jj: e.matmul(PS[4][:, :], DW[:, ct * 4 + jj, :], XL[ct][0][:, jj:jj + 512], start=(jj == 0), stop=(jj == 3)),
                             r=[XL[ct][1], "DW"], w=[("ps", 4)])
                    P.op("act", lambda e, L=L, ct=ct: e.activation(L["XC"][0], PS[4][:, :], AF.Identity, bias=PAR[:, P_CONVB + ct:P_CONVB + ct + 1]),
                         r=[("ps", 4), "PAR"], w=[L["XC"][1]])
                    P.op("dve", lambda e, ct=ct: e.tensor_copy(XL[ct][0][:, 0:3], XL[ct][0][:, 512:515]), r=[XL[ct][1]], w=[XL[ct][1]], c=0.15)
                    P.op("dve", lambda e, L=L: e.tensor_copy(L["XCB"][0], L["XC"][0]), r=[L["XC"][1]], w=[L["XCB"][1]], c=0.4)
                    gb = 4
                    P.op("pe", lambda e, L=L, ct=ct, gb=gb: e.matmul(PS[gb][:, :], WBD[:, ct, :], L["XCB"][0], start=True, stop=True),
                         r=["WBD", L["XCB"][1]], w=[("ps", gb)])
                    P.op("pe", lambda e, L=L, ct=ct, gb=gb: e.matmul(PS[gb + 1][:, :], WBD[:, 4 + ct, :], L["XCB"][0], start=True, stop=True),
                         r=["WBD", L["XCB"][1]], w=[("ps", gb + 1)])
                    P.op("act", lambda e, L=L, ct=ct, gb=gb: e.activation(L["MM"][0], PS[gb][:, :], AF.Tanh, bias=sm(S_HBA + ct), scale=0.5),
                         r=[("ps", gb), "HC"], w=[L["MM"][1]])
                    P.op("act", lambda e, L=L, ct=ct, gb=gb: e.activation(L["BB"][0], PS[gb + 1][:, :], AF.Tanh, bias=sm(S_HBI + ct), scale=0.5),
                         r=[("ps", gb + 1), "HC"], w=[L["BB"][1]])
                    P.op("act", lambda e, L=L, ct=ct: e.activation(L["AA"][0], L["MM"][0], AF.Exp, bias=sm(S_HC + ct), scale=sm(S_HC + ct)),
                         r=[L["MM"][1], "HC"], w=[L["AA"][1]])
                    P.op("act", lambda e, L=L, ct=ct: e.activation(L["MM"][0], L["MM"][0], AF.Exp, bias=sm(S_HC2 + ct), scale=sm(S_HC2 + ct)),
                         r=[L["MM"][1], "HC"], w=[L["MM"][1]])
                    P.op("dve", lambda e, L=L: e.scalar_tensor_tensor(L["BB"][0], L["BB"][0], 1.0, L["XC"][0], ALU.add, ALU.mult),
                         r=[L["BB"][1], L["XC"][1]], w=[L["BB"][1]])
                mm0 = LS[0]["MM"][0]
                mm1 = LS[1]["MM"][0]
                mm_pair = bass.AP(mm0.tensor, mm0.offset, [list(mm0.ap[0]), [mm1.offset - mm0.offset, 2], [1, 512]])
                P.op("act", lambda e, mm_pair=mm_pair: e.activation(mm_pair, mm_pair, AF.Sqrt, bias=sm(S_ONE), scale=-1.0),
                     r=[LS[0]["MM"][1], LS[1]["MM"][1], "SMc"], w=[LS[0]["MM"][1], LS[1]["MM"][1]], c=1.1)
                for cc in range(2):
                    ct = cp * 2 + cc
                    L = LS[(cp * 2 + cc) % NLS]
                    if T == 0:
                        P.op("dve", lambda e, L=L: e.memset(L["MM"][0][:, 0:1], 1.0), w=[L["MM"][1]])
                    P.op("dve", lambda e, L=L: e.scalar_tensor_tensor(L["BB"][0], L["BB"][0], 0.5, L["MM"][0], ALU.mult, ALU.mult),
                         r=[L["BB"][1], L["MM"][1]], w=[L["BB"][1]])
                    init = sm(S_ZERO) if T == 0 else sm(S_HST + ct)
                    P.op("dve", lambda e, L=L, init=init: e.tensor_tensor_scan(L["HH"][0], L["AA"][0], L["BB"][0], init, ALU.mult, ALU.add),
                         r=[L["AA"][1], L["BB"][1], ("HST", ct), "SMc", L["GS"][1]], w=[L["HH"][1]])
                    P.op("dve", lambda e, L=L, ct=ct: e.tensor_copy(sm(S_HST + ct), L["HH"][0][:, 511:512]), r=[L["HH"][1]], w=[("HST", ct)], c=0.15)
                    P.op("dve", lambda e, L=L, ct=ct, l0=l0: e.tensor_tensor(YL[:, ct, l0:l0 + 512], L["HH"][0], L["GS"][0], ALU.mult),
                         r=[L["HH"][1], L["GS"][1]], w=[("YL", ct)])

            def emit_qk(qk, hps=range(4)):
                for hp in hps:
                    ft = 8 + qk * 4 + hp
                    par = (qk * 4 + hp) % 2
                    bank = win_fm(ft)
                    dstk = qk_key(qk * 4 + hp)
                    p6 = ("ps", 5)
                    P.op("act", lambda e, bank=bank, par=par: e.activation(SQ[par][0], PS[bank][:, :], AF.Square), r=[("ps", bank)], w=[SQ[par][1]])
                    P.op("dve", lambda e, bank=bank, par=par: e.tensor_copy(QC[par][0], PS[bank][:, :]), r=[("ps", bank)], w=[QC[par][1]])
                    for u in range(4):
                        P.op("pe", lambda e, u=u, par=par: e.matmul(PS[5][:, par * 8 + u * 2:par * 8 + u * 2 + 2], SQ[par][0][:, u * 128:(u + 1) * 128], IND2,
                                                                    start=True, stop=True),
                             r=[SQ[par][1], "CB"], w=[p6], c=0.06)
                    ssqs = sm(S_SSQS + par * 8, S_SSQS + par * 8 + 8)
                    rsc = sm(S_RSC + par * 8, S_RSC + par * 8 + 8)
                    P.op("dve", lambda e, par=par, ssqs=ssqs: e.tensor_scalar(ssqs, PS[5][:, par * 8:par * 8 + 8], 1.0 / 64, EPS, ALU.mult, ALU.add),
                         r=[p6], w=[("SSQS", par)], c=0.15)
                    nh_b = bass.AP(SM[:].tensor, sm(S_NHALF).offset, [list(sm(S_NHALF).ap[0]), [0, 8]])
                    P.op("pool", lambda e, nh_b=nh_b, ssqs=ssqs, rsc=rsc: e.tensor_tensor(rsc, ssqs, nh_b, ALU.pow),
                         r=[("SSQS", par), "SMc"], w=[("RSC", par)], c=1.3)
                    rsc_b = bass.AP(SM[:].tensor, rsc.offset, [list(rsc.ap[0]), [1, 8], [0, 64]])
                    xh = RSXH[0].rearrange("p (a d) -> p a d", d=64)
                    P.op("dve", lambda e, rsc_b=rsc_b, xh=xh: e.tensor_copy(xh, rsc_b), r=[("RSC", par)], w=[RSXH[1]], c=0.35)
                    for u in range(4):
                        P.op("pe", lambda e, u=u: e.matmul(PS[7][:, u * 128:(u + 1) * 128], RSXH[0][:, u * 128:(u + 1) * 128], IDENT, start=True, stop=True),
                             r=[RSXH[1], "CB"], w=[("ps", 7)], c=0.07)
                    col = sm(S_QCOL) if qk == 0 else sm(S_ONE)
                    dst = QKG[:, qk * 4 + hp, l0:l0 + 512]
                    P.op("dve", lambda e, par=par, col=col, dst=dst: e.scalar_tensor_tensor(dst, QC[par][0], col, PS[7][:, :], ALU.mult, ALU.mult),
                         r=[("ps", 7), QC[par][1], "HC", "SMc"], w=[dstk])
                    if qk == 1:
                        kview = QKG[:, 4 + hp, l0:l0 + 512].rearrange("p (b n) -> p b n", b=2)
                        kms = SM[:, S_KMS + hp * 8 + T * 2:S_KMS + hp * 8 + T * 2 + 2]
                        P.op("dve", lambda e, kview=kview, kms=kms: e.tensor_reduce(kms, kview, AX.X, ALU.add), r=[dstk], w=[("KMS", hp)])
            def emit_ga(hps=range(4)):
                for hp in hps:
                    bank = win_fm(20 + hp)
                    P.op("act", lambda e, bank=bank: e.activation(TGA[0], PS[bank][:, :], AF.Tanh, scale=0.5), r=[("ps", bank)], w=[TGA[1]])
                    P.op("dve", lambda e, bank=bank, hp=hp, l0=l0: e.scalar_tensor_tensor(QKG[:, 8 + hp, l0:l0 + 512], TGA[0], 1.0, PS[bank][:, :], ALU.add, ALU.mult),
                         r=[("ps", bank), TGA[1]], w=[qk_key(8 + hp)])
            def emit_v(us=range(4)):
                for u in us:
                    bank = (0, 1, 2, 6)[zrot[0] % 4]
                    zrot[0] += 1
                    for kt in range(8):
                        P.op("pe", lambda e, u=u, kt=kt, bank=bank, HT=HT: e.matmul(PS[bank][:, :], HT[:, kt, u * 128:(u + 1) * 128], WI[:, kt, 2048:2560],
                                                                          start=(kt == 0), stop=(kt == 7)),
                             r=[("WI", 4), HT_k], w=[("ps", bank)])
                    P.op("act", lambda e, u=u, bank=bank, T=T: e.activation(VT[:, T * 4 + u, :], PS[bank][:, :], AF.Copy), r=[("ps", bank)], w=[("VT", T * 4 + u)])

            for item in TILE_ORDER:
                if item[0] == 'l':
                    emit_lru(int(item[1]))
                elif item[0] in 'qk':
                    emit_qk(0 if item[0] == 'q' else 1, [int(c_) for c_ in item[1:]] if len(item) > 1 else range(4))
                elif item[0] == 'g':
                    emit_ga([int(c_) for c_ in item[1:]] if len(item) > 1 else range(4))
                elif item[0] == 'v':
                    emit_v([int(c_) for c_ in item[1:]] if len(item) > 1 else range(4))

        if not do_b:
            continue
        P.ph = 'B%d.pre' % s
        for ft in range(8):
            P.op("dve", lambda e, ft=ft, s=s: e.tensor_scalar(DG[:, ft, :], IDF[:], ADAF[:, 16 + ft, s:s + 1], 0.5, ALU.mult, ALU.mult),
                 r=["IDF", "ADAFg"], w=[DG_k], c=0.2)
        for nh in range(2):
            for q4 in range(4):
                ft = nh * 4 + q4
                P.op("pe", lambda e, nh=nh, q4=q4, ft=ft: e.matmul(PS[nh][:, q4 * 128:(q4 + 1) * 128], ONESF[:], DG[:, ft, :], start=True, stop=True),
                     r=["IDF", DG_k], w=[("ps", nh)], c=0.25)
            P.op("act", lambda e, nh=nh: e.activation(XT[:, 1, nh * 512:(nh + 1) * 512], PS[nh][:, :], AF.Copy), r=[("ps", nh)], w=[("XT", 1)])
        for kt in range(8):
            P.op("sp", lambda e, kt=kt: e.dma_start(out=XT[:, 0, :], in_=wout_d[kt * 128:(kt + 1) * 128, :]), w=[("XT", 0)], dma="xt0")
            P.op("dve", lambda e, kt=kt: e.tensor_tensor(WO[:, kt, :], XT[:, 0, :], XT[:, 1, :], ALU.mult),
                 r=[("XT", 0), ("XT", 1)], w=[WO_k], c=1.1)
        for hp in range(4):
            P.op("dve", lambda e, hp=hp: e.tensor_scalar(KMX[0:64, hp, 0:8], SM[0:64, S_KMS + hp * 8:S_KMS + hp * 8 + 8], 1.0 / 256, None, ALU.mult),
                 r=[("KMS", hp)], w=["KMX"])
            P.op("dve", lambda e, hp=hp: e.tensor_scalar(KMX[64:128, hp, 8:16], SM[64:128, S_KMS + hp * 8:S_KMS + hp * 8 + 8], 1.0 / 256, None, ALU.mult),
                 r=[("KMS", hp)], w=["KMX"])
        for q8 in range(8):
            qt = 8 + q8
            for hp in range(4):
                P.op("pe", lambda e, q8=q8, qt=qt, hp=hp: e.matmul(PS[7][:, q8 * 64 + hp * 16:q8 * 64 + hp * 16 + 16],
                                                              QKG[:, hp, qt * 128:(qt + 1) * 128], KMX[:, hp, :], start=True, stop=True),
                     r=[qk_key(hp), "KMX"], w=[("ps", 7)], c=0.06)
        P.op("dve", lambda e: e.tensor_copy(GSS[0], PS[7][:, :]), r=[("ps", 7)], w=[GSS[1]])
        P.op("dve", lambda e: e.memset(SELa, 0.0), w=[SEL_k])
        P.op("dve", lambda e: e.memset(UNSa, 0.0), w=[UNS_k])
        for j in range(4, 8):
            qa = 2 * (j - 4)

            def v3(ap_, j=j, qa=qa):
                return ap_[:, qa * 64:(qa + 2) * 64].rearrange("p (g i) -> p g i", i=8)[:, :, 0:j]

            def t3(k, j=j):
                return STMP[k][0].rearrange("p (g i) -> p g i", i=8)[:, :, 0:j]

            def m2(k):
                return STMP[k][0][:, 0:16]

            def mb(k, j=j):
                a_ = STMP[k][0][:, 0:16]
                return bass.AP(a_.tensor, a_.offset, [list(a_.ap[0]), [1, 16], [0, j]])
            Gj = v3(GSS[0])
            sk = [STMP[k][1] for k in range(6)]
            P.op("dve", lambda e, Gj=Gj: e.tensor_reduce(m2(0), Gj, AX.X, ALU.max), r=[GSS[1]], w=[sk[0]])
            P.op("dve", lambda e, Gj=Gj, mb=mb, t3=t3: e.tensor_tensor(t3(1), Gj, mb(0), ALU.is_ge), r=[GSS[1], sk[0]], w=[sk[1]])
            P.op("dve", lambda e, Gj=Gj, t3=t3: e.scalar_tensor_tensor(t3(2), t3(1), NEG, Gj, ALU.mult, ALU.add), r=[GSS[1], sk[1]], w=[sk[2]])
            P.op("dve", lambda e, t3=t3: e.tensor_reduce(m2(3), t3(2), AX.X, ALU.max), r=[sk[2]], w=[sk[3]])
            P.op("dve", lambda e, mb=mb, t3=t3: e.tensor_tensor(t3(1), t3(2), mb(3), ALU.is_ge), r=[sk[2], sk[3]], w=[sk[1]])
            P.op("dve", lambda e, t3=t3: e.scalar_tensor_tensor(t3(4), t3(1), NEG, t3(2), ALU.mult, ALU.add), r=[sk[2], sk[1]], w=[sk[4]])
            P.op("dve", lambda e, t3=t3: e.tensor_reduce(m2(5), t3(4), AX.X, ALU.max), r=[sk[4]], w=[sk[5]])
            P.op("dve", lambda e, Gj=Gj, mb=mb, t3=t3: e.tensor_tensor(t3(1), Gj, mb(5), ALU.is_ge), r=[GSS[1], sk[5]], w=[sk[1]])
            SELj = SELa[:, qa * 64:(qa + 2) * 64].rearrange("p (g i) -> p g i", i=8)[:, :, 0:j]
            P.op("dve", lambda e, SELj=SELj, t3=t3: e.tensor_copy(SELj, t3(1)), r=[sk[1]], w=[SEL_k])
            for half in range(2):
                for q2 in range(2):
                    dstu = UNS[:, qa + q2, :, half * 32:half * 32 + 16].rearrange("p h (l i) -> p h l i", i=8)[:, :, :, 0:j]
                    srcu = STMP[1][0][:, q2 * 64:(q2 + 1) * 64].rearrange("p (h l i) -> p h l i", l=2, i=8)[:, :, :, 0:j]
                    P.op("dve", lambda e, dstu=dstu, srcu=srcu: e.tensor_scalar(dstu, srcu, -1.0, 1.0, ALU.mult, ALU.add), r=[sk[1]], w=[UNS_k])
        for hp in range(4):
            for t in range(16):
                i = t // 2
                P.op("pe", lambda e, hp=hp, t=t, i=i: e.matmul(PS[4][0:48, hp * 128:(hp + 1) * 128], CB[:, C_ONES4 + 7 - i:C_ONES4 + 55 - i],
                                                          VT[:, t, hp * 128:(hp + 1) * 128], start=(t == 0), stop=(t == 15)),
                     r=[("VT", t), "CB"], w=[("ps", 4)], c=0.08)
        nbm_b = bass.AP(NBM[:].tensor, NBM[0:48, :].offset, [list(NBM[0:48, :].ap[0]), [0, 4], [1, 128]])
        v1 = VST1[0][0:48, :].rearrange("p (h n) -> p h n", h=4)
        vh = VSHI[0][0:48, :].rearrange("p (h n) -> p h n", h=4)
        P.op("dve", lambda e, nbm_b=nbm_b, v1=v1: e.tensor_tensor(v1, PS[4][0:48, :].rearrange("p (h n) -> p h n", h=4), nbm_b, ALU.mult),
             r=[("ps", 4), "NBM"], w=[VST1[1]])
        P.op("dve", lambda e: e.tensor_copy(VSHI[0][0:48, :], VST1[0][0:48, :]), r=[VST1[1]], w=[VSHI[1]])
        P.op("dve", lambda e: e.tensor_tensor(NVS[32:48, :, :], VST1[0][32:48, :].rearrange("p (h n) -> p h n", h=4),
                                            VSHI[0][32:48, :].rearrange("p (h n) -> p h n", h=4), ALU.subtract),
             r=[VST1[1], VSHI[1]], w=["NVS"])
        P.op("dve", lambda e: e.tensor_copy(NVS[0:16, :, :], VSHI[0][0:16, :].rearrange("p (h n) -> p h n", h=4)), r=[VSHI[1]], w=["NVS"])

        sgrp = [0]
        for j in range(8):
            P.ph = 'B%d.%d' % (s, j)
            jb = j % 2
            for hp in range(4):
                qsl = QKG[:, hp, j * 256:(j + 1) * 256]
                ob = 4 + (j * 4 + hp) % 2
                SELT = SELT2[(j * 4 + hp) % 2]
                UNST = UNST2[(j * 4 + hp) % 2]
                QP, QP_k = QP2[(j * 4 + hp) % 2]
                RD = RD2[(j * 4 + hp) % 2]
                T2 = T22[(j * 4 + hp) % 2]
                if j >= 4:
                    qa = 2 * (j - 4)
                    for q2 in range(2):
                        P.op("pe", lambda e, qa=qa, q2=q2, hp=hp, ob=ob: e.transpose(psb(ob)[0:16, q2 * 128:(q2 + 1) * 128], SEL[:, qa + q2, hp, :], IDENT),
                             r=[SEL_k, "CB"], w=[("ps", ob)], c=0.08)
                        P.op("pe", lambda e, qa=qa, q2=q2, hp=hp, ob=ob: e.transpose(psb(ob)[0:48, 256 + q2 * 128:256 + (q2 + 1) * 128], UNS[:, qa + q2, hp, :], IDENT),
                             r=[UNS_k, "CB"], w=[("ps", ob)], c=0.08)
                    P.op("dve", lambda e, SELT=SELT, ob=ob: e.tensor_copy(SELT[0][0:16, :], psb(ob)[0:16, 0:256]), r=[("ps", ob)], w=[SELT[1]], c=0.2)
                    P.op("dve", lambda e, UNST=UNST, ob=ob: e.tensor_copy(UNST[0][0:48, :], psb(ob)[0:48, 256:512]), r=[("ps", ob)], w=[UNST[1]], c=0.2)
                    for i0 in range(0, j, 2):
                        ni = min(2, j - i0)
                        for ii in range(ni):
                            i = i0 + ii
                            P.op("pe", lambda e, i=i, ii=ii, SELT=SELT, ob=ob: e.matmul(PS[ob][:, ii * 256:(ii + 1) * 256], CB[0:16, C_ESEL + i * 128:C_ESEL + (i + 1) * 128],
                                                                              SELT[0][0:16, :], start=True, stop=True),
                                 r=[SELT[1], "CB"], w=[("ps", ob)], c=0.14)
                        q_b = bass.AP(qsl.tensor, qsl.offset, [list(qsl.ap[0]), [0, ni], [1, 256]])
                        P.op("dve", lambda e, i0=i0, ni=ni, q_b=q_b, QP=QP, ob=ob: e.tensor_tensor(
                            QP[:, i0:i0 + ni, :], q_b, PS[ob][:, 0:ni * 256].rearrange("p (i n) -> p i n", i=ni), ALU.mult),
                            r=[("ps", ob), qk_key(hp)], w=[QP_k], c=0.4 * ni)
                    P.op("pe", lambda e, hp=hp, ob=ob, UNST=UNST: e.matmul(PS[ob][:, 0:256], NVS[0:48, hp, :], UNST[0][0:48, :], start=True, stop=False),
                         r=["NVS", UNST[1]], w=[("ps", ob)], c=0.14)
                    P.op("pe", lambda e, ob=ob, UNST=UNST: e.matmul(PS[ob][:, 256:512], DCOR, UNST[0][0:48, :], start=False, stop=False),
                         r=["CB", UNST[1]], w=[("ps", ob)], c=0.14)
                tiles = []
                for i in range(j):
                    tiles.append((2 * i, 0, 256, i))
                    tiles.append((2 * i + 1, 0, 256, i))
                tiles.append((2 * j, 0, 256, j))
                tiles.append((2 * j + 1, 128, 256, j))
                ntile = len(tiles)
                first = [j < 4]
                for g in range(0, ntile, 2):
                    b = sgrp[0] % 2
                    sgrp[0] += 1
                    ba, bb = b * 2, b * 2 + 1
                    grp = tiles[g:g + 2]
                    for slot, (t, qlo, qhi, i) in enumerate(grp):
                        rhs_t = QP[:, i, :] if (j >= 4 and i < j) else qsl
                        rk = [QP_k] if (j >= 4 and i < j) else [qk_key(hp)]
                        dg_ = (i == j)
                        P.op("pe", lambda e, ba=ba, slot=slot, t=t, qlo=qlo, qhi=qhi, rhs_t=rhs_t, hp=hp, dg_=dg_: e.matmul(
                            PS[ba][:, slot * 256 + qlo:slot * 256 + qhi], QKG[0:64, 4 + hp, t * 128:(t + 1) * 128], rhs_t[0:64, qlo:qhi], start=True, stop=not dg_),
                            r=rk + [qk_key(4 + hp)], w=[("ps", ba)], c=0.19, md="r")
                        P.op("pe", lambda e, bb=bb, slot=slot, t=t, qlo=qlo, qhi=qhi, rhs_t=rhs_t, hp=hp, dg_=dg_: e.matmul(
                            PS[bb][:, slot * 256 + qlo:slot * 256 + qhi], QKG[64:128, 4 + hp, t * 128:(t + 1) * 128], rhs_t[64:128, qlo:qhi], start=True, stop=not dg_),
                            r=rk + [qk_key(4 + hp)], w=[("ps", bb)], c=0.02, md="r")
                        if dg_:
                            mc = slot * 256 + (0 if t % 2 == 0 else 128)
                            seq_ = [(ba, 0, 0, False), (bb, 64, 0, False), (ba, 0, 1, True), (bb, 64, 1, True)]
                            for (bk_, r0_, var_, last_) in seq_:
                                lw_ = CB[r0_:r0_ + 64, C_IDENT:C_IDENT + 128] if var_ == 0 else MASKN[r0_:r0_ + 64, 1, :]
                                rh_ = MASKN[r0_:r0_ + 64, 0, :] if var_ == 0 else MASKN[r0_:r0_ + 64, 2, :]
                                P.op("pe", lambda e, bk_=bk_, mc=mc, last_=last_, lw_=lw_, rh_=rh_: e.matmul(
                                    PS[bk_][:, mc:mc + 128], lw_, rh_, start=False, stop=last_),
                                    r=["CB"], w=[("ps", bk_), "mchain"], c=0.06, md="r")
                    diag = (grp[0][3] == j)
                    if not diag:
                        P.op("act", lambda e, b=b, ba=ba: e.activation(PTAB[b], PSALL[:, ba * 512:(ba + 2) * 512], AF.Exp),
                             r=[("ps", ba), ("ps", bb)], w=[PTA[b][1], PTBb[b][1]], c=1.05)
                    ranges = [(0, 256), (384, 512)] if diag else []
                    for (c0, c1) in ranges:
                        P.op("act", lambda e, b=b, ba=ba, c0=c0, c1=c1: e.activation(PTA[b][0][:, c0:c1], PS[ba][:, c0:c1], AF.Exp),
                             r=[("ps", ba)], w=[PTA[b][1]])
                        P.op("act", lambda e, b=b, bb=bb, c0=c0, c1=c1: e.activation(PTBb[b][0][:, c0:c1], PS[bb][:, c0:c1], AF.Exp),
                             r=[("ps", bb)], w=[PTBb[b][1]])
                    for slot, (t, qlo, qhi, i) in enumerate(grp):
                        st = first[0]
                        first[0] = False
                        last = (g + slot == ntile - 1)
                        pa = PTA[b][0][:, slot * 256 + qlo:slot * 256 + qhi]
                        pb = PTBb[b][0][:, slot * 256 + qlo:slot * 256 + qhi]
                        P.op("pe", lambda e, t=t, qlo=qlo, qhi=qhi, pa=pa, st=st, last=last, hp=hp, ob=ob: e.matmul(
                            PS[ob][0:64, qlo:qhi], VT[:, t, hp * 128:hp * 128 + 64], pa, start=st, stop=False),
                            r=[("VT", t), PTA[b][1]], w=[("ps", ob)], c=0.19, md="c")
                        P.op("pe", lambda e, t=t, qlo=qlo, qhi=qhi, pb=pb, st=st, last=last, hp=hp, ob=ob: e.matmul(
                            PS[ob][64:128, qlo:qhi], VT[:, t, hp * 128 + 64:hp * 128 + 128], pb, start=st, stop=False),
                            r=[("VT", t), PTBb[b][1]], w=[("ps", ob)], c=0.02, md="c")
                        P.op("pe", lambda e, qlo=qlo, qhi=qhi, pa=pa, last=last, ob=ob: e.matmul(
                            PS[ob][0:64, 256 + qlo:256 + qhi], ONES64, pa, start=False, stop=last), r=["CB", PTA[b][1]], w=[("ps", ob)], c=0.19, md="c")
                        P.op("pe", lambda e, qlo=qlo, qhi=qhi, pb=pb, last=last, ob=ob: e.matmul(
                            PS[ob][64:128, 256 + qlo:256 + qhi], ONES64, pb, start=False, stop=last), r=["CB", PTBb[b][1]], w=[("ps", ob)], c=0.02, md="c")
                OD = OD2[(j * 4 + hp) % 2]
                P.op("dve", lambda e, ob=ob, OD=OD: e.tensor_copy(OD[0], PS[ob][:, :]), r=[("ps", ob)], w=[OD[1]])
                P.op("dve", lambda e, OD=OD, RD=RD: e.reciprocal(RD[0], OD[0][:, 256:512]), r=[OD[1]], w=[RD[1]], c=1.8)
                P.op("dve", lambda e, hp=hp, j=j, RD=RD, T2=T2: e.tensor_tensor(T2[0], RD[0], QKG[:, 8 + hp, j * 256:(j + 1) * 256], ALU.mult),
                     r=[RD[1], qk_key(8 + hp)], w=[T2[1]], c=0.4)
                P.op("dve", lambda e, hp=hp, jb=jb, OD=OD, T2=T2: e.tensor_tensor(YA[jb][0][:, hp, :], OD[0][:, 0:256], T2[0], ALU.mult),
                     r=[OD[1], T2[1]], w=[YA[jb][1]], c=0.4)
            for tt in range(2):
                xs = tt
                tok = s * SEQ + j * 256 + tt * 128
                ltok = j * 256 + tt * 128
                P.op("sp", lambda e, xs=xs, tok=tok: e.dma_start(out=XT[:, xs, :], in_=x_d[tok:tok + 128, :]), w=[("XT", xs)], dma="xt%d" % xs)
                for nh in range(2):
                    wb = 6 + nh
                    for kt in range(8):
                        lhs = YL[:, kt, ltok:ltok + 128] if kt < 4 else YA[jb][0][:, kt - 4, tt * 128:(tt + 1) * 128]
                        rk = [("YL", kt)] if kt < 4 else [YA[jb][1]]
                        P.op("pe", lambda e, lhs=lhs, kt=kt, nh=nh, wb=wb: e.matmul(PS[wb][:, :], lhs, WO[:, kt, nh * 512:(nh + 1) * 512], start=(kt == 0), stop=(kt == 7)),
                             r=rk + [WO_k], w=[("ps", wb)])
                    P.op("dve", lambda e, xs=xs, nh=nh, wb=wb: e.tensor_tensor(XT[:, xs, nh * 512:(nh + 1) * 512], PS[wb][:, :], XT[:, xs, nh * 512:(nh + 1) * 512], ALU.add),
                         r=[("ps", wb), ("XT", xs)], w=[("XT", xs)])
                P.op("sp", lambda e, xs=xs, tok=tok: e.dma_start(out=out_d[tok:tok + 128, :], in_=XT[:, xs, :]), r=[("XT", xs)], dma="out%d" % xs)

    if debug:
        dumps = [("d_SM", SM[:], [128, 256], F32, []),
                 ("d_QKG", QKG[:].rearrange("p a b -> p (a b)"), [128, 12 * 2048], BF16, []),
                 ("d_VT", VT[:].rearrange("p a b -> p (a b)"), [128, 16 * 512], BF16, []),
                 ("d_YL", YL[:].rearrange("p a b -> p (a b)"), [128, 4 * 2048], BF16, []),
                 ("d_OV", OV[:], [128, OVW], F32, []), ("d_NVS", NVS[:].rearrange("p a b -> p (a b)"), [128, 512], BF16, []),
                 ("d_KMX", KMX[:].rearrange("p a b -> p (a b)"), [128, 64], BF16, [])]
        allk = set()
        for o_ in P.ops:
            allk.update(o_["r"]); allk.update(o_["w"])
        for (nm, ap_, shp, dt_, _) in dumps:
            dd = nc.dram_tensor(nm, shp, dt_, kind="ExternalOutput").ap()
            P.op("sp", lambda e, dd=dd, ap_=ap_: e.dma_start(out=dd[:, :], in_=ap_), r=list(allk), dma="out_" + nm)
    P.finalize()
    sems = {}
    for sk in P.semkeys:
        sems[sk] = es.enter_context(nc.semaphore("s_" + "_".join(str(x) for x in sk)))
    by_eng = {}
    for o in P.ops:
        by_eng.setdefault(o["eng"], []).append(o)

    def emit(engname, e):
        for o in by_eng.get(engname, []):
            for (sk, v) in o["waits"]:
                e.wait_ge(sems[sk], v)
            ins = o["fn"](e)
            if o["sem"] is not None:
                ins.then_inc(sems[o["sem"]], o["inc"])
        if engname == "sp":
            for k, v in P.final_dma.items():
                if k.startswith("out"):
                    e.wait_ge(sems[("dma", k)], v)

    with nc.Block() as block:
        @block.tensor
        def _(e):
            emit("pe", e)

        @block.scalar
        def _(e):
            emit("act", e)

        @block.vector
        def _(e):
            emit("dve", e)

        @block.gpsimd
        def _(e):
            emit("pool", e)

        @block.sync
        def _(e):
            emit("sp", e)
    es.close()
    return nc


def _host_inputs(x, c, w_ada, b_ada, norm_g, w_in, conv_w, conv_b, lru_wa, lru_ba, lru_wi, lru_bi,
                 lru_lambda, q_norm_g, k_norm_g, w_out):
    f = np.float32
    x = np.asarray(x, f); c = np.asarray(c, f)
    w_ada = np.ascontiguousarray(np.asarray(w_ada, f)[0]); b_ada = np.asarray(b_ada, f)[0]
    norm_g = np.asarray(norm_g, f)[0]; w_in = np.ascontiguousarray(np.asarray(w_in, f)[0])
    conv_w = np.asarray(conv_w, f)[0]; conv_b = np.asarray(conv_b, f)[0]
    lru_wa = np.asarray(lru_wa, f)[0]; lru_wi = np.asarray(lru_wi, f)[0]
    lru_ba = np.asarray(lru_ba, f)[0]; lru_bi = np.asarray(lru_bi, f)[0]; lam = np.asarray(lru_lambda, f)[0]
    gq = np.asarray(q_norm_g, f)[0]; gk = np.asarray(k_norm_g, f)[0]
    w_out = np.ascontiguousarray(np.asarray(w_out, f)[0])
    con = _consts()
    wbd = np.zeros((128, 8, 128), f)
    for ct in range(4):
        for g in range(2):
            wbd[g * 64:(g + 1) * 64, ct, g * 64:(g + 1) * 64] = lru_wa[2 * ct + g]
            wbd[g * 64:(g + 1) * 64, 4 + ct, g * 64:(g + 1) * 64] = lru_wi[2 * ct + g]
    wbd = wbd.reshape(128, 1024)
    dwc = np.zeros((128, 16, 128), f)
    for ct in range(4):
        for j in range(4):
            dwc[np.arange(128), ct * 4 + j, np.arange(128)] = conv_w[j, ct * 128:(ct + 1) * 128]
    dwc = dwc.reshape(128, 2048)
    maps = []
    for core in range(NCORES):
        par = np.zeros((128, NPAR), f)
        cc = c[core * NSEQ:(core + 1) * NSEQ]
        par[:, P_CT:P_CT + 16] = cc.reshape(NSEQ, 8, 128).transpose(2, 0, 1).reshape(128, 16)
        par[:, P_BADA:P_BADA + 24] = b_ada.reshape(24, 128).T
        par[:, P_NORMG:P_NORMG + 8] = norm_g.reshape(8, 128).T
        par[:, P_CONVW:P_CONVW + 16] = conv_w.reshape(4, 4, 128).transpose(2, 1, 0).reshape(128, 16)
        par[:, P_CONVB:P_CONVB + 4] = conv_b.reshape(4, 128).T
        par[:, P_LBA:P_LBA + 4] = lru_ba.reshape(4, 128).T
        par[:, P_LBI:P_LBI + 4] = lru_bi.reshape(4, 128).T
        par[:, P_LAM:P_LAM + 4] = lam.reshape(4, 128).T
        par[:, P_GQ] = np.tile(gq, 2)
        par[:, P_GK] = np.tile(gk, 2)
        maps.append({
            "x": np.ascontiguousarray(x[core * NSEQ:(core + 1) * NSEQ].reshape(NSEQ * SEQ, D)),
            "par": par, "con": con, "w_ada": w_ada, "w_in": w_in, "wbd": wbd, "w_out": w_out, "dwc": dwc,
        })
    return maps


def kernel(**inputs):
    maps = _host_inputs(**inputs)
    nc = build_program()
    res = run_bass_kernel_spmd(nc, maps, core_ids=list(range(NCORES)))
    outs = [np.asarray(r["out"]).reshape(NSEQ, SEQ, D) for r in res.results]
    return np.concatenate(outs, axis=0).astype(np.float32)
```

```python
import contextlib
import numpy as np
import concourse.bass as bass
import concourse.mybir as mybir
from concourse.bass_utils import run_bass_kernel_spmd

F32 = mybir.dt.float32
BF16 = mybir.dt.bfloat16
AF = mybir.ActivationFunctionType
ALU = mybir.AluOpType
AX = mybir.AxisListType

NCORES = 8
SEQ = 2048
D = 1024
NSEQ = 2
EPS = 1e-6
NEG = -1.0e30

C_IDENT, C_TRI, C_ESEL, C_ONES4, C_DCOR, C_ONES64, C_IND2, C_IND4, C_NBM, NCON = (
    0, 128, 256, 1280, 1336, 1464, 1528, 1530, 1658, 2170)
NCB = 1658
C_MASKN = 1786
P_CT, P_BADA, P_NORMG, P_CONVW, P_CONVB, P_LBA, P_LBI, P_LAM, P_GQ, P_GK, NPAR = (
    0, 16, 40, 48, 64, 68, 72, 76, 80, 81, 82)


def _consts():
    c = np.zeros((128, NCON), np.float32)
    c[:, C_IDENT:C_IDENT + 128] = np.eye(128, dtype=np.float32)
    k = np.arange(128)[:, None]
    q = np.arange(128)[None, :]
    c[:, C_TRI:C_TRI + 128] = (k <= q).astype(np.float32)
    mk = np.where(k > q, -30000.0, 0.0).astype(np.float32)
    c[:, C_MASKN:C_MASKN + 128] = mk
    isw = np.zeros((128, 128), np.float32)
    isw[np.arange(128), (np.arange(128) + 64) % 128] = 1.0
    c[:, C_MASKN + 128:C_MASKN + 256] = isw
    c[:, C_MASKN + 256:C_MASKN + 384] = np.concatenate([mk[64:128], mk[0:64]], axis=0)
    for i in range(8):
        for hl in range(2):
            c[hl * 8 + i, C_ESEL + i * 128 + hl * 64: C_ESEL + i * 128 + hl * 64 + 64] = 1.0
    for col in (7, 15, 39, 47):
        c[:, C_ONES4 + col] = 1.0
    for r in range(16):
        hl = r // 8
        c[r, C_DCOR + hl * 64: C_DCOR + hl * 64 + 64] = -256.0
        c[r, C_NBM + hl * 64: C_NBM + hl * 64 + 64] = -1.0
        c[32 + r, C_NBM + hl * 64: C_NBM + hl * 64 + 64] = -1.0
    c[:, C_ONES64:C_ONES64 + 64] = 1.0
    c[0:64, C_IND2] = 1.0
    c[64:128, C_IND2 + 1] = 1.0
    for r in range(4):
        c[r, C_IND4 + (r % 2) * 64: C_IND4 + (r % 2) * 64 + 64] = 1.0
    return c


class Prog:
    def __init__(self):
        self.ops = []

    DEFC = {"pe": 0.25, "act": 0.65, "dve": 0.65, "pool": 1.0, "sp": 0.1}

    def op(self, eng, fn, r=(), w=(), dma=None, c=None, tb=None, md=None, nb=524288):
        rk, wk = [], []
        for b in r:
            rk.extend(b if isinstance(b, list) else [b])
        for b in w:
            wk.extend(b if isinstance(b, list) else [b])
        if c is None:
            c = self.DEFC[eng]
        if eng == "pe":
            tb = md or "f"
        if eng == "act" and tb is None:
            names = fn.__code__.co_names
            if "Exp" in names or "Tanh" in names:
                tb = "e"
            elif "Sqrt" in names:
                tb = "s"
            elif "Ln" in names:
                tb = "l"
        self.ops.append(dict(eng=eng, fn=fn, r=rk, w=wk, dma=dma, c=c, tb=tb, ph=getattr(self, 'ph', ''), nb=nb))

    def schedule(self):
        ops = self.ops
        n = len(ops)
        succ = [[] for _ in range(n)]
        indeg = [0] * n
        for i, o in enumerate(ops):
            for d in o["deps"]:
                succ[d].append(i)
                indeg[i] += 1
        LAT = getattr(self, "LAT", 0.35)
        prio = [0.0] * n
        for i in range(n - 1, -1, -1):
            m = 0.0
            for s_ in succ[i]:
                if prio[s_] > m:
                    m = prio[s_]
            prio[i] = ops[i]["c"] + (3.0 if ops[i]["dma"] is not None else 0.0) + m
        for i in range(n):
            if ops[i]["dma"] is not None and not ops[i]["deps"]:
                prio[i] += 1e6 - i
        rt = [0.0] * n
        ready = {}
        efree = {}
        table = [None]
        pmode = ["f"]
        dma_free = [0.0]
        for i in range(n):
            efree.setdefault(ops[i]["eng"], 0.0)
            if indeg[i] == 0:
                ready.setdefault(ops[i]["eng"], []).append(i)
        order = []
        done = 0
        while done < n:
            best = None
            for eng, lst in ready.items():
                if not lst:
                    continue
                t_e = efree[eng]
                avail = [i for i in lst if rt[i] <= t_e]
                if avail:
                    if eng == "act":
                        same = [i for i in avail if ops[i]["tb"] is None or ops[i]["tb"] == table[0]]
                        if same:
                            avail = same
                    elif eng == "pe":
                        same = [i for i in avail if ops[i]["tb"] == pmode[0]]
                        if same:
                            avail = same
                    pick = max(avail, key=lambda i: (prio[i], -i)) if getattr(self, 'PRIO', 'cp') == 'cp' else min(avail)
                    st = t_e
                else:
                    pick = min(lst, key=lambda i: (rt[i], -prio[i]))
                    st = rt[pick]
                if best is None or st < best[0]:
                    best = (st, eng, pick)
            st, eng, i = best
            ready[eng].remove(i)
            o = ops[i]
            dur = o["c"]
            if eng == "act" and o["tb"] is not None and o["tb"] != table[0]:
                dur += 2.6
                table[0] = o["tb"]
            if eng == "pe" and o["tb"] != pmode[0]:
                dur += 0.12
                pmode[0] = o["tb"]
            fin = st + dur
            efree[eng] = fin
            o['t0'] = st
            o['t1'] = fin
            if o["dma"] is not None:
                dstart = max(fin + 1.5, dma_free[0])
                vis = dstart + o["nb"] / 250e3
                dma_free[0] = vis
            else:
                vis = fin
            order.append(i)
            done += 1
            for s_ in succ[i]:
                lat = 0.0 if (ops[s_]["eng"] == "pe" and eng == "pe" and o["dma"] is None) else LAT
                if vis + lat > rt[s_]:
                    rt[s_] = vis + lat
                indeg[s_] -= 1
                if indeg[s_] == 0:
                    ready.setdefault(ops[s_]["eng"], []).append(s_)
        remap = {old: new for new, old in enumerate(order)}
        newops = [ops[i] for i in order]
        for o in newops:
            o["deps"] = {remap[d] for d in o["deps"]}
        self.ops = newops
        self.sim_time = max(efree.values())

    def finalize(self):
        ops = self.ops
        last_w, readers = {}, {}
        for i, o in enumerate(ops):
            deps = set()
            for k in o["r"]:
                if k in last_w:
                    deps.add(last_w[k])
            for k in o["w"]:
                if k in last_w:
                    deps.add(last_w[k])
                for rr in readers.get(k, ()):
                    deps.add(rr)
            deps.discard(i)
            o["deps"] = deps
            for k in o["w"]:
                last_w[k] = i
                readers[k] = []
            for k in o["r"]:
                if k not in o["w"]:
                    readers.setdefault(k, []).append(i)
        if getattr(self, "do_sched", True):
            self.schedule()
            ops = self.ops
        needs = [False] * len(ops)
        for i, o in enumerate(ops):
            for d in o["deps"]:
                dd = ops[d]
                if dd["dma"] is None and not (dd["eng"] == "pe" and o["eng"] == "pe" and o["dma"] is None):
                    needs[d] = True
        cnt, dcnt = {}, {}
        for i, o in enumerate(ops):
            if o["dma"] is not None:
                dcnt[o["dma"]] = dcnt.get(o["dma"], 0) + 16
                o["sem"] = ("dma", o["dma"])
                o["val"] = dcnt[o["dma"]]
                o["inc"] = 16
            elif needs[i]:
                cnt[o["eng"]] = cnt.get(o["eng"], 0) + 1
                o["sem"] = ("eng", o["eng"])
                o["val"] = cnt[o["eng"]]
                o["inc"] = 1
            else:
                o["sem"] = None
        waited = {}
        for i, o in enumerate(ops):
            need = {}
            for d in o["deps"]:
                dd = ops[d]
                if dd["sem"] is None:
                    continue
                if dd["dma"] is None and dd["eng"] == "pe" and o["eng"] == "pe" and o["dma"] is None:
                    continue
                need[dd["sem"]] = max(need.get(dd["sem"], 0), dd["val"])
            wl = []
            for sk, v in need.items():
                key = (o["eng"], sk)
                if waited.get(key, 0) >= v:
                    continue
                waited[key] = v
                wl.append((sk, v))
            o["waits"] = wl
        self.final_dma = dict(dcnt)
        self.semkeys = sorted({o["sem"] for o in ops if o["sem"] is not None}, key=str)


def build_program(debug=False, nseq_run=NSEQ, do_b=True, NLS=2, OVW=14656, TILE_ORDER=('q', 'l0', 'v', 'k', 'l1', 'g')):
    nc = bass.Bass("TRN2", target_bir_lowering=False)
    ntok = NSEQ * SEQ
    x_d = nc.dram_tensor("x", [ntok, D], F32, kind="ExternalInput").ap()
    par_d = nc.dram_tensor("par", [128, NPAR], F32, kind="ExternalInput").ap()
    con_d = nc.dram_tensor("con", [128, NCON], F32, kind="ExternalInput").ap()
    wada_d = nc.dram_tensor("w_ada", [D, 3 * D], F32, kind="ExternalInput").ap()
    win_d = nc.dram_tensor("w_in", [D, 3 * D], F32, kind="ExternalInput").ap()
    wbd_d = nc.dram_tensor("wbd", [128, 1024], F32, kind="ExternalInput").ap()
    wout_d = nc.dram_tensor("w_out", [D, D], F32, kind="ExternalInput").ap()
    dwc_d = nc.dram_tensor("dwc", [128, 2048], F32, kind="ExternalInput").ap()
    out_d = nc.dram_tensor("out", [ntok, D], F32, kind="ExternalOutput").ap()

    P = Prog()
    es = contextlib.ExitStack()

    def sb(name, shape, dt):
        return es.enter_context(nc.sbuf_tensor(name, shape, dt))

    WI = sb("WI", [128, 8, 3072], BF16)
    QKG = sb("QKG", [128, 12, 2048], BF16)
    VT = sb("VT", [128, 16, 512], BF16)
    YL = sb("YL", [128, 4, 2048], BF16)
    IDF = sb("IDF", [128, 128], F32)
    ONESF = sb("ONESF", [128, 128], F32)
    WBD = sb("WBD", [128, 8, 128], BF16)
    DW = sb("DW", [128, 16, 128], BF16)
    PAR = sb("PAR", [128, NPAR], F32)
    SM = sb("SM", [128, 256], F32)
    SMB = sb("SMB", [128, 64], BF16)
    CB = sb("CB", [128, NCB], BF16)
    NBM = sb("NBM", [128, 128], F32)
    MASKN = sb("MASKN", [128, 3, 128], BF16)
    XT = sb("XT", [128, 2, 1024], F32)
    NVS = sb("NVS", [128, 4, 128], BF16)
    KMX = sb("KMX", [128, 4, 16], BF16)
    OV = sb("OV", [128, OVW], F32)
    PSALL = es.enter_context(nc.psum_tensor("PSALL", [128, 4096], F32))
    PS = [PSALL[:, i * 512:(i + 1) * 512] for i in range(8)]

    WADA = QKG[:].rearrange("p a b -> p (a b)").rearrange("p (k n) -> p k n", k=8)
    YLX = YL[:].rearrange("p a b -> p (a b)").bitcast(F32)

    def ovf(off, n):
        return OV[:, off:off + n], [("ov", pg) for pg in range(off // 64, (off + n + 63) // 64)]

    def ovb(off, n):
        assert n % 2 == 0
        return OV[:, off:off + n // 2].bitcast(BF16), [("ov", pg) for pg in range(off // 64, (off + n // 2 + 63) // 64)]

    o = 0
    HN = []
    for i in range(2):
        HN.append(ovb(o, 1024)); o += 512
    HT2 = []
    for i in range(2):
        a_, k_ = ovb(o, 8 * 512); o += 2048
        HT2.append((a_.rearrange("p (k n) -> p k n", k=8), k_))
    XL = []
    for i in range(4):
        XL.append(ovb(o, 520)); o += 260
    LS = []
    lru_base = o
    for i in range(NLS):
        d_ = {}
        for nm in ("XC", "AA", "MM", "BB", "HH"):
            d_[nm] = ovf(o, 512); o += 512
        d_["GS"] = ovb(o, 512); o += 256
        d_["XCB"] = ovb(o, 512); o += 256
        LS.append(d_)
    SQ = []
    QC = []
    for i in range(2):
        SQ.append(ovb(o, 512)); o += 256
        QC.append(ovf(o, 512)); o += 512
    TGA = ovf(o, 512); o += 512
    RSXH = ovb(o, 512); o += 256
    assert o <= OVW, o
    CST = ovf(lru_base, NCON)
    so = lru_base
    assert so <= OVW
    o = 0
    GSS = ovf(o, 512); o += 512
    STMP = []
    for i in range(6):
        STMP.append(ovf(o, 128)); o += 128
    SELa, SEL_k = ovb(o, 8 * 64); o += 256
    SEL = SELa.rearrange("p (q h c) -> p q h c", q=8, h=4)
    UNSa, UNS_k = ovb(o, 8 * 4 * 48); o += 768
    UNS = UNSa.rearrange("p (q h c) -> p q h c", q=8, h=4)
    VST1 = ovf(o, 512); o += 512
    VSHI = ovb(o, 512); o += 256
    dg_off = o
    DG_ap, DG_k = ovf(o, 1024); o += 1024
    DG = DG_ap.rearrange("p (f n) -> p f n", f=8)
    SELT2 = []
    UNST2 = []
    for i in range(2):
        SELT2.append(ovb(o, 256)); o += 128
        UNST2.append(ovb(o, 256)); o += 128
    od0 = ovf(o, 512); o += 512
    assert o <= 5120, o
    o = 5120
    WO_ap, WO_k = ovb(o, 8192); o += 4096
    WO = WO_ap.rearrange("p (k n) -> p k n", k=8)
    QP2 = []
    for i in range(2):
        a_, k_ = ovb(o, 8 * 256); o += 1024
        QP2.append((a_.rearrange("p (i n) -> p i n", i=8), k_))
    PTA = []
    PTBb = []
    PTAB = []
    for i in range(2):
        PTAB.append(OV[:, o:o + 512].bitcast(BF16))
        PTA.append(ovb(o, 512)); o += 256
        PTBb.append(ovb(o, 512)); o += 256
    RD2 = []
    T22 = []
    for i in range(2):
        RD2.append(ovf(o, 256)); o += 256
        T22.append(ovf(o, 256)); o += 256
    YA = []
    for i in range(2):
        a_, k_ = ovb(o, 1024); o += 512
        YA.append((a_.rearrange("p (h n) -> p h n", h=4), k_))
    OD2 = [od0, ovf(dg_off, 512)]
    assert o <= OVW, o

    def sm(a, b=None):
        return SM[:, a:(a + 1 if b is None else b)]
    S_TC, S_ADAF, S_AMOD, S_HC, S_HC2, S_HBA, S_HBI, S_QCOL, S_KCOL, S_ONE, S_NHALF = 0, 192, 48, 64, 68, 72, 76, 80, 81, 82, 83
    S_EX, S_SP, S_SSQ, S_VV, S_RSTD, S_HST, S_KMS, S_SSQS, S_RSC, S_ZERO = 84, 88, 96, 100, 104, 108, 112, 144, 160, 176
    SCB = SMB[:, 0:16]
    RHL = SMB[:, 16:32]

    def cb(off, n, rows=128):
        return CB[0:rows, off:off + n]
    IDENT = cb(C_IDENT, 128)
    TRI = cb(C_TRI, 128)
    ONES64 = cb(C_ONES64, 64)
    IND2 = cb(C_IND2, 2)
    IND4 = cb(C_IND4, 128, 4)
    DCOR = cb(C_DCOR, 128, 48)

    def psb(i):
        return PS[i][:].bitcast(BF16)

    P.op("sp", lambda e: e.dma_start(out=PAR[:], in_=par_d[:, :]), w=["PAR"], dma="par", nb=65536)
    P.op("sp", lambda e: e.dma_start(out=CST[0], in_=con_d[:, :]), w=[CST[1]], dma="cst")
    qkg_keys = [("QKG", i) for i in range(12)]
    vt_keys = [("VT", i) for i in range(16)]
    VTW = VT[:].rearrange("p a b -> p (a b)").rearrange("p (k n) -> p k n", k=8)
    for kt in range(8):
        P.op("pool", lambda e, kt=kt: e.dma_start(out=WADA[:, kt, 0:2048], in_=wada_d[kt * 128:(kt + 1) * 128, 0:2048]),
             w=[("WADA", kt)] + (qkg_keys if kt == 0 else []), dma="wada%d" % kt, nb=1 << 20)
    win_v = win_d.rearrange("(k p) n -> p k n", p=128)
    for cbk in (2, 0, 1, 3, 5, 4):
        P.op("pool", lambda e, cbk=cbk: e.dma_start(out=WI[:, :, cbk * 512:(cbk + 1) * 512], in_=win_v[:, :, cbk * 512:(cbk + 1) * 512]),
             w=[("WI", cbk)], dma="wi%d" % cbk, nb=2 << 20)
        if cbk == 2:
            P.op("pool", lambda e: e.dma_start(out=WBD[:].rearrange("p a b -> p (a b)"), in_=wbd_d[:, :]), w=["WBD"], dma="wbd")
            P.op("pool", lambda e: e.dma_start(out=DW[:].rearrange("p a b -> p (a b)"), in_=dwc_d[:, :]), w=["DW"], dma="dwc", nb=1 << 20)
    for kt in range(8):
        P.op("pool", lambda e, kt=kt: e.dma_start(out=VTW[:, kt, :], in_=wada_d[kt * 128:(kt + 1) * 128, 2048:3072]),
             w=[("WADAg", kt)] + (vt_keys if kt == 0 else []), dma="wadag%d" % kt)
    P.op("dve", lambda e: e.tensor_copy(CB[:], CST[0][:, 0:NCB]), r=[CST[1]], w=["CB"])
    P.op("dve", lambda e: e.tensor_copy(NBM[:], CST[0][:, C_NBM:C_NBM + 128]), r=[CST[1]], w=["NBM"])
    P.op("dve", lambda e: e.tensor_copy(MASKN[:].rearrange("p a b -> p (a b)"), CST[0][:, C_MASKN:C_MASKN + 384]), r=[CST[1]], w=["CB"])
    P.op("dve", lambda e: e.tensor_copy(IDF[:], CST[0][:, C_IDENT:C_IDENT + 128]), r=[CST[1]], w=["IDF"])
    P.op("dve", lambda e: e.memset(ONESF[:], 1.0), w=["IDF"])
    P.op("dve", lambda e: e.memset(sm(S_ONE), 1.0), w=["SMc"])
    P.op("dve", lambda e: e.memset(sm(S_NHALF), -0.5), w=["SMc"])
    P.op("dve", lambda e: e.memset(sm(S_ZERO), 0.0), w=["SMc"])
    P.op("dve", lambda e: e.memset(NVS[:], 0.0), w=["NVS"])
    P.op("dve", lambda e: e.memset(KMX[:], 0.0), w=["KMX"])
    P.op("act", lambda e: e.activation(sm(S_EX, S_EX + 4), PAR[:, P_LAM:P_LAM + 4], AF.Exp, scale=-1.0), r=["PAR"], w=["EX"])
    P.op("act", lambda e: e.activation(sm(S_SP, S_SP + 4), sm(S_EX, S_EX + 4), AF.Ln, bias=sm(S_ONE)), r=["EX", "SMc"], w=["SP"])
    P.op("dve", lambda e: e.tensor_scalar(sm(S_HC, S_HC + 4), sm(S_SP, S_SP + 4), -4.0, None, ALU.mult), r=["SP"], w=["HC"])
    P.op("dve", lambda e: e.tensor_scalar(sm(S_HC2, S_HC2 + 4), sm(S_SP, S_SP + 4), -8.0, None, ALU.mult), r=["SP"], w=["HC"])
    P.op("dve", lambda e: e.tensor_scalar(sm(S_HBA, S_HBA + 4), PAR[:, P_LBA:P_LBA + 4], 0.5, None, ALU.mult), r=["PAR"], w=["HC"])
    P.op("dve", lambda e: e.tensor_scalar(sm(S_HBI, S_HBI + 4), PAR[:, P_LBI:P_LBI + 4], 0.5, None, ALU.mult), r=["PAR"], w=["HC"])
    P.op("dve", lambda e: e.scalar_tensor_tensor(sm(S_QCOL), PAR[:, P_GQ:P_GQ + 1], 0.125, PAR[:, P_GK:P_GK + 1], ALU.mult, ALU.mult),
         r=["PAR"], w=["HC"])
    P.op("act", lambda e: e.activation(sm(S_TC, S_TC + 16), PAR[:, P_CT:P_CT + 16], AF.Tanh, scale=0.5), r=["PAR"], w=["TC"])
    P.op("dve", lambda e: e.scalar_tensor_tensor(SCB, sm(S_TC, S_TC + 16), 1.0, PAR[:, P_CT:P_CT + 16], ALU.add, ALU.mult),
         r=["TC", "PAR"], w=["SCB"])
    ADP = PS[0]
    for ft in range(24):
        for kt in range(8):
            if ft < 16:
                P.op("pe", lambda e, ft=ft, kt=kt: e.matmul(ADP[:, ft * 2:ft * 2 + 2], WADA[:, kt, ft * 128:(ft + 1) * 128],
                                                          SMB[:, kt:kt + 9:8], start=(kt == 0), stop=(kt == 7)),
                     r=[("WADA", kt), ("WADA", 7), "SCB"] + qkg_keys, w=[("ps", 0)], c=0.08)
            else:
                P.op("pe", lambda e, ft=ft, kt=kt: e.matmul(PS[1][:, (ft - 16) * 2:(ft - 16) * 2 + 2], VTW[:, kt, (ft - 16) * 128:(ft - 15) * 128],
                                                          SMB[:, kt:kt + 9:8], start=(kt == 0), stop=(kt == 7)),
                     r=[("WADAg", kt), ("WADAg", 7), "SCB"] + vt_keys, w=[("ps", 1)], c=0.08)
    ADAF = SM[:, S_ADAF:S_ADAF + 48].rearrange("p (f b) -> p f b", b=2)
    bada_g = bass.AP(PAR[:].tensor, PAR[:, P_BADA + 16:P_BADA + 24].offset,
                     [list(PAR[:, P_BADA + 16:P_BADA + 24].ap[0]), [1, 8], [0, 2]])
    bada_b = bass.AP(PAR[:].tensor, PAR[:, P_BADA:P_BADA + 16].offset,
                     [list(PAR[:, P_BADA:P_BADA + 16].ap[0]), [1, 16], [0, 2]])
    P.op("dve", lambda e: e.scalar_tensor_tensor(ADAF[:, 16:24, :], PS[1][:, 0:16].rearrange("p (f b) -> p f b", b=2), 0.5, bada_g, ALU.mult, ALU.add),
         r=[("ps", 1), "PAR"], w=["ADAFg"])
    P.op("dve", lambda e: e.scalar_tensor_tensor(ADAF[:, 0:16, :], ADP[:, 0:32].rearrange("p (f b) -> p f b", b=2), 0.5, bada_b, ALU.mult, ALU.add),
         r=[("ps", 0), "PAR"], w=["ADAF"])
    ng_b = bass.AP(PAR[:].tensor, PAR[:, P_NORMG:P_NORMG + 8].offset, [list(PAR[:, P_NORMG:P_NORMG + 8].ap[0]), [1, 8], [0, 2]])
    AMOD = SM[:, S_AMOD:S_AMOD + 16].rearrange("p (f b) -> p f b", b=2)
    P.op("dve", lambda e: e.scalar_tensor_tensor(AMOD, ADAF[:, 8:16, :], 1.0, ng_b, ALU.add, ALU.mult), r=["ADAF", "PAR"], w=["AMOD"])

    def qk_key(i):
        return ("QKG", i)

    for s in range(nseq_run):
        for T in range(4):
            P.ph = 'A%d.%d' % (s, T)
            HT, HT_k = HT2[T % 2]
            t0 = s * SEQ + T * 512
            l0 = T * 512
            for u in range(4):
                xs = u % 2
                tok = t0 + u * 128
                ptb = [3, 4, 5, 7][u] if (s == 0 and T == 0) else 3
                if s == 0 and T == 0:
                    xsrc = YLX[:, u * 1024:(u + 1) * 1024]
                    xk = [("YL", c_) for c_ in range(4)] + [("YLX", u)]
                    P.op("sp", lambda e, xsrc=xsrc, tok=tok: e.dma_start(out=xsrc, in_=x_d[tok:tok + 128, :]), w=[("YLX", u)], dma="xy%d" % u)
                else:
                    xsrc = XT[:, xs, :]
                    xk = [("XT", xs)]
                    P.op("sp", lambda e, xs=xs, tok=tok: e.dma_start(out=XT[:, xs, :], in_=x_d[tok:tok + 128, :]), w=xk, dma="xt%d" % xs)
                P.op("act", lambda e, xs=xs, u=u, xsrc=xsrc: e.activation(HN[xs][0], xsrc, AF.Square, accum_out=sm(S_SSQ + u)),
                     r=xk, w=[HN[xs][1], ("SSQ", u)], c=1.1)
                P.op("dve", lambda e, u=u: e.tensor_scalar(sm(S_VV + u), sm(S_SSQ + u), 1.0 / D, EPS, ALU.mult, ALU.add),
                     r=[("SSQ", u)], w=[("VV", u)], c=0.15)
                P.op("pool", lambda e, u=u: e.tensor_tensor(sm(S_RSTD + u), sm(S_VV + u), sm(S_NHALF), ALU.pow),
                     r=[("VV", u), "SMc"], w=[("RSTD", u)], c=1.3)
                P.op("dve", lambda e, xs=xs, u=u, xsrc=xsrc: e.tensor_scalar(HN[xs][0], xsrc, sm(S_RSTD + u), None, ALU.mult),
                     r=xk + [("RSTD", u)], w=[HN[xs][1]], c=0.65)
                for kt in range(8):
                    P.op("pe", lambda e, xs=xs, kt=kt, ptb=ptb: e.transpose(psb(ptb)[:, kt * 128:(kt + 1) * 128], HN[xs][0][:, kt * 128:(kt + 1) * 128], IDENT),
                         r=[HN[xs][1], "CB"], w=[("ps", ptb)], c=0.08)
                for kt in range(8):
                    src = psb(ptb)[:, kt * 128:(kt + 1) * 128]
                    dst = HT[:, kt, u * 128:(u + 1) * 128]
                    if kt % 4 != 3:
                        P.op("dve", lambda e, src=src, dst=dst, kt=kt, s=s: e.tensor_scalar(dst, src, AMOD[:, kt, s:s + 1], ADAF[:, kt, s:s + 1], ALU.mult, ALU.add),
                             r=[("ps", ptb), "AMOD", "ADAF"], w=[HT_k], c=0.3)
                    else:
                        P.op("act", lambda e, src=src, dst=dst, kt=kt, s=s: e.activation(dst, src, AF.Identity, bias=ADAF[:, kt, s:s + 1], scale=AMOD[:, kt, s:s + 1]),
                             r=[("ps", ptb), "AMOD", "ADAF"], w=[HT_k], c=0.3)

            zrot = [0]

            def win_fm(ft):
                bank = (0, 1, 2, 6)[zrot[0] % 4]
                zrot[0] += 1
                for kt in range(8):
                    P.op("pe", lambda e, ft=ft, kt=kt, bank=bank, HT=HT: e.matmul(PS[bank][:, :], WI[:, kt, ft * 128:(ft + 1) * 128], HT[:, kt, :],
                                                                       start=(kt == 0), stop=(kt == 7)),
                         r=[("WI", ft // 4), HT_k], w=[("ps", bank)])
                return bank

            def emit_lru(cp):
                for cc in range(2):
                    ct = cp * 2 + cc
                    L = LS[(cp * 2 + cc) % NLS]
                    bank = win_fm(ct)
                    P.op("act", lambda e, ct=ct, bank=bank: e.activation(XL[ct][0][:, 3:515], PS[bank][:, :], AF.Copy),
                         r=[("ps", bank)], w=[XL[ct][1]])
                    if T == 0:
                        P.op("dve", lambda e, ct=ct: e.memset(XL[ct][0][:, 0:3], 0.0), w=[XL[ct][1]])
                    bank = win_fm(4 + ct)
                    P.op("act", lambda e, L=L, bank=bank: e.activation(L["HH"][0], PS[bank][:, :], AF.Tanh, scale=0.5),
                         r=[("ps", bank)], w=[L["HH"][1]])
                    P.op("dve", lambda e, L=L, bank=bank: e.scalar_tensor_tensor(L["GS"][0], L["HH"][0], 1.0, PS[bank][:, :], ALU.add, ALU.mult),
                         r=[("ps", bank), L["HH"][1]], w=[L["GS"][1]])
                    for jj in range(4):
                        P.op("pe", lambda e, ct=ct, jj=jj: e.matmul(PS[4][:, :], DW[:, ct * 4 + jj, :], XL[ct][0][:, jj:jj + 512], start=(jj == 0), stop=(jj == 3)),
                             r=[XL[ct][1], "DW"], w=[("ps", 4)])
                    P.op("act", lambda e, L=L, ct=ct: e.activation(L["XC"][0], PS[4][:, :], AF.Identity, bias=PAR[:, P_CONVB + ct:P_CONVB + ct + 1]),
                         r=[("ps", 4), "PAR"], w=[L["XC"][1]])
                    P.op("dve", lambda e, ct=ct: e.tensor_copy(XL[ct][0][:, 0:3], XL[ct][0][:, 512:515]), r=[XL[ct][1]], w=[XL[ct][1]], c=0.15)
                    P.op("dve", lambda e, L=L: e.tensor_copy(L["XCB"][0], L["XC"][0]), r=[L["XC"][1]], w=[L["XCB"][1]], c=0.4)
                    gb = 4
                    P.op("pe", lambda e, L=L, ct=ct, gb=gb: e.matmul(PS[gb][:, :], WBD[:, ct, :], L["XCB"][0], start=True, stop=True),
                         r=["WBD", L["XCB"][1]], w=[("ps", gb)])
                    P.op("pe", lambda e, L=L, ct=ct, gb=gb: e.matmul(PS[gb + 1][:, :], WBD[:, 4 + ct, :], L["XCB"][0], start=True, stop=True),
                         r=["WBD", L["XCB"][1]], w=[("ps", gb + 1)])
                    P.op("act", lambda e, L=L, ct=ct, gb=gb: e.activation(L["MM"][0], PS[gb][:, :], AF.Tanh, bias=sm(S_HBA + ct), scale=0.5),
                         r=[("ps", gb), "HC"], w=[L["MM"][1]])
                    P.op("act", lambda e, L=L, ct=ct, gb=gb: e.activation(L["BB"][0], PS[gb + 1][:, :], AF.Tanh, bias=sm(S_HBI + ct), scale=0.5),
                         r=[("ps", gb + 1), "HC"], w=[L["BB"][1]])
                    P.op("act", lambda e, L=L, ct=ct: e.activation(L["AA"][0], L["MM"][0], AF.Exp, bias=sm(S_HC + ct), scale=sm(S_HC + ct)),
                         r=[L["MM"][1], "HC"], w=[L["AA"][1]])
                    P.op("act", lambda e, L=L, ct=ct: e.activation(L["MM"][0], L["MM"][0], AF.Exp, bias=sm(S_HC2 + ct), scale=sm(S_HC2 + ct)),
                         r=[L["MM"][1], "HC"], w=[L["MM"][1]])
                    P.op("dve", lambda e, L=L: e.scalar_tensor_tensor(L["BB"][0], L["BB"][0], 1.0, L["XC"][0], ALU.add, ALU.mult),
                         r=[L["BB"][1], L["XC"][1]], w=[L["BB"][1]])
                mm0 = LS[0]["MM"][0]
                mm1 = LS[1]["MM"][0]
                mm_pair = bass.AP(mm0.tensor, mm0.offset, [list(mm0.ap[0]), [mm1.offset - mm0.offset, 2], [1, 512]])
                P.op("act", lambda e, mm_pair=mm_pair: e.activation(mm_pair, mm_pair, AF.Sqrt, bias=sm(S_ONE), scale=-1.0),
                     r=[LS[0]["MM"][1], LS[1]["MM"][1], "SMc"], w=[LS[0]["MM"][1], LS[1]["MM"][1]], c=1.1)
                for cc in range(2):
                    ct = cp * 2 + cc
                    L = LS[(cp * 2 + cc) % NLS]
                    if T == 0:
                        P.op("dve", lambda e, L=L: e.memset(L["MM"][0][:, 0:1], 1.0), w=[L["MM"][1]])
                    P.op("dve", lambda e, L=L: e.scalar_tensor_tensor(L["BB"][0], L["BB"][0], 0.5, L["MM"][0], ALU.mult, ALU.mult),
                         r=[L["BB"][1], L["MM"][1]], w=[L["BB"][1]])
                    init = sm(S_ZERO) if T == 0 else sm(S_HST + ct)
                    P.op("dve", lambda e, L=L, init=init: e.tensor_tensor_scan(L["HH"][0], L["AA"][0], L["BB"][0], init, ALU.mult, ALU.add),
                         r=[L["AA"][1], L["BB"][1], ("HST", ct), "SMc", L["GS"][1]], w=[L["HH"][1]])
                    P.op("dve", lambda e, L=L, ct=ct: e.tensor_copy(sm(S_HST + ct), L["HH"][0][:, 511:512]), r=[L["HH"][1]], w=[("HST", ct)], c=0.15)
                    P.op("dve", lambda e, L=L, ct=ct, l0=l0: e.tensor_tensor(YL[:, ct, l0:l0 + 512], L["HH"][0], L["GS"][0], ALU.mult),
                         r=[L["HH"][1], L["GS"][1]], w=[("YL", ct)])

            def emit_qk(qk, hps=range(4)):
                for hp in hps:
                    ft = 8 + qk * 4 + hp
                    par = (qk * 4 + hp) % 2
                    bank = win_fm(ft)
                    dstk = qk_key(qk * 4 + hp)
                    p6 = ("ps", 5)
                    P.op("act", lambda e, bank=bank, par=par: e.activation(SQ[par][0], PS[bank][:, :], AF.Square), r=[("ps", bank)], w=[SQ[par][1]])
                    P.op("dve", lambda e, bank=bank, par=par: e.tensor_copy(QC[par][0], PS[bank][:, :]), r=[("ps", bank)], w=[QC[par][1]])
                    for u in range(4):
                        P.op("pe", lambda e, u=u, par=par: e.matmul(PS[5][:, par * 8 + u * 2:par * 8 + u * 2 + 2], SQ[par][0][:, u * 128:(u + 1) * 128], IND2,
                                                                    start=True, stop=True),
                             r=[SQ[par][1], "CB"], w=[p6], c=0.06)
                    ssqs = sm(S_SSQS + par * 8, S_SSQS + par * 8 + 8)
                    rsc = sm(S_RSC + par * 8, S_RSC + par * 8 + 8)
                    P.op("dve", lambda e, par=par, ssqs=ssqs: e.tensor_scalar(ssqs, PS[5][:, par * 8:par * 8 + 8], 1.0 / 64, EPS, ALU.mult, ALU.add),
                         r=[p6], w=[("SSQS", par)], c=0.15)
                    nh_b = bass.AP(SM[:].tensor, sm(S_NHALF).offset, [list(sm(S_NHALF).ap[0]), [0, 8]])
                    P.op("pool", lambda e, nh_b=nh_b, ssqs=ssqs, rsc=rsc: e.tensor_tensor(rsc, ssqs, nh_b, ALU.pow),
                         r=[("SSQS", par), "SMc"], w=[("RSC", par)], c=1.3)
                    rsc_b = bass.AP(SM[:].tensor, rsc.offset, [list(rsc.ap[0]), [1, 8], [0, 64]])
                    xh = RSXH[0].rearrange("p (a d) -> p a d", d=64)
                    P.op("dve", lambda e, rsc_b=rsc_b, xh=xh: e.tensor_copy(xh, rsc_b), r=[("RSC", par)], w=[RSXH[1]], c=0.35)
                    for u in range(4):
                        P.op("pe", lambda e, u=u: e.matmul(PS[7][:, u * 128:(u + 1) * 128], RSXH[0][:, u * 128:(u + 1) * 128], IDENT, start=True, stop=True),
                             r=[RSXH[1], "CB"], w=[("ps", 7)], c=0.07)
                    col = sm(S_QCOL) if qk == 0 else sm(S_ONE)
                    dst = QKG[:, qk * 4 + hp, l0:l0 + 512]
                    P.op("dve", lambda e, par=par, col=col, dst=dst: e.scalar_tensor_tensor(dst, QC[par][0], col, PS[7][:, :], ALU.mult, ALU.mult),
                         r=[("ps", 7), QC[par][1], "HC", "SMc"], w=[dstk])
                    if qk == 1:
                        kview = QKG[:, 4 + hp, l0:l0 + 512].rearrange("p (b n) -> p b n", b=2)
                        kms = SM[:, S_KMS + hp * 8 + T * 2:S_KMS + hp * 8 + T * 2 + 2]
                        P.op("dve", lambda e, kview=kview, kms=kms: e.tensor_reduce(kms, kview, AX.X, ALU.add), r=[dstk], w=[("KMS", hp)])
            def emit_ga(hps=range(4)):
                for hp in hps:
                    bank = win_fm(20 + hp)
                    P.op("act", lambda e, bank=bank: e.activation(TGA[0], PS[bank][:, :], AF.Tanh, scale=0.5), r=[("ps", bank)], w=[TGA[1]])
                    P.op("dve", lambda e, bank=bank, hp=hp, l0=l0: e.scalar_tensor_tensor(QKG[:, 8 + hp, l0:l0 + 512], TGA[0], 1.0, PS[bank][:, :], ALU.add, ALU.mult),
                         r=[("ps", bank), TGA[1]], w=[qk_key(8 + hp)])
            def emit_v(us=range(4)):
                for u in us:
                    bank = (0, 1, 2, 6)[zrot[0] % 4]
                    zrot[0] += 1
                    for kt in range(8):
                        P.op("pe", lambda e, u=u, kt=kt, bank=bank, HT=HT: e.matmul(PS[bank][:, :], HT[:, kt, u * 128:(u + 1) * 128], WI[:, kt, 2048:2560],
                                                                          start=(kt == 0), stop=(kt == 7)),
                             r=[("WI", 4), HT_k], w=[("ps", bank)])
                    P.op("act", lambda e, u=u, bank=bank, T=T: e.activation(VT[:, T * 4 + u, :], PS[bank][:, :], AF.Copy), r=[("ps", bank)], w=[("VT", T * 4 + u)])

            for item in TILE_ORDER:
                if item[0] == 'l':
                    emit_lru(int(item[1]))
                elif item[0] in 'qk':
                    emit_qk(0 if item[0] == 'q' else 1, [int(c_) for c_ in item[1:]] if len(item) > 1 else range(4))
                elif item[0] == 'g':
                    emit_ga([int(c_) for c_ in item[1:]] if len(item) > 1 else range(4))
                elif item[0] == 'v':
                    emit_v([int(c_) for c_ in item[1:]] if len(item) > 1 else range(4))

        if not do_b:
            continue
        P.ph = 'B%d.pre' % s
        for ft in range(8):
            P.op("dve", lambda e, ft=ft, s=s: e.tensor_scalar(DG[:, ft, :], IDF[:], ADAF[:, 16 + ft, s:s + 1], 0.5, ALU.mult, ALU.mult),
                 r=["IDF", "ADAFg"], w=[DG_k], c=0.2)
        for nh in range(2):
            for q4 in range(4):
                ft = nh * 4 + q4
                P.op("pe", lambda e, nh=nh, q4=q4, ft=ft: e.matmul(PS[nh][:, q4 * 128:(q4 + 1) * 128], ONESF[:], DG[:, ft, :], start=True, stop=True),
                     r=["IDF", DG_k], w=[("ps", nh)], c=0.25)
            P.op("act", lambda e, nh=nh: e.activation(XT[:, 1, nh * 512:(nh + 1) * 512], PS[nh][:, :], AF.Copy), r=[("ps", nh)], w=[("XT", 1)])
        for kt in range(8):
            P.op("sp", lambda e, kt=kt: e.dma_start(out=XT[:, 0, :], in_=wout_d[kt * 128:(kt + 1) * 128, :]), w=[("XT", 0)], dma="xt0")
            P.op("dve", lambda e, kt=kt: e.tensor_tensor(WO[:, kt, :], XT[:, 0, :], XT[:, 1, :], ALU.mult),
                 r=[("XT", 0), ("XT", 1)], w=[WO_k], c=1.1)
        for hp in range(4):
            P.op("dve", lambda e, hp=hp: e.tensor_scalar(KMX[0:64, hp, 0:8], SM[0:64, S_KMS + hp * 8:S_KMS + hp * 8 + 8], 1.0 / 256, None, ALU.mult),
                 r=[("KMS", hp)], w=["KMX"])
            P.op("dve", lambda e, hp=hp: e.tensor_scalar(KMX[64:128, hp, 8:16], SM[64:128, S_KMS + hp * 8:S_KMS + hp * 8 + 8], 1.0 / 256, None, ALU.mult),
                 r=[("KMS", hp)], w=["KMX"])
        for q8 in range(8):
            qt = 8 + q8
            for hp in range(4):
                P.op("pe", lambda e, q8=q8, qt=qt, hp=hp: e.matmul(PS[7][:, q8 * 64 + hp * 16:q8 * 64 + hp * 16 + 16],
                                                              QKG[:, hp, qt * 128:(qt + 1) * 128], KMX[:, hp, :], start=True, stop=True),
                     r=[qk_key(hp), "KMX"], w=[("ps", 7)], c=0.06)
        P.op("dve", lambda e: e.tensor_copy(GSS[0], PS[7][:, :]), r=[("ps", 7)], w=[GSS[1]])
        P.op("dve", lambda e: e.memset(SELa, 0.0), w=[SEL_k])
        P.op("dve", lambda e: e.memset(UNSa, 0.0), w=[UNS_k])
        for j in range(4, 8):
            qa = 2 * (j - 4)

            def v3(ap_, j=j, qa=qa):
                return ap_[:, qa * 64:(qa + 2) * 64].rearrange("p (g i) -> p g i", i=8)[:, :, 0:j]

            def t3(k, j=j):
                return STMP[k][0].rearrange("p (g i) -> p g i", i=8)[:, :, 0:j]

            def m2(k):
                return STMP[k][0][:, 0:16]

            def mb(k, j=j):
                a_ = STMP[k][0][:, 0:16]
                return bass.AP(a_.tensor, a_.offset, [list(a_.ap[0]), [1, 16], [0, j]])
            Gj = v3(GSS[0])
            sk = [STMP[k][1] for k in range(6)]
            P.op("dve", lambda e, Gj=Gj: e.tensor_reduce(m2(0), Gj, AX.X, ALU.max), r=[GSS[1]], w=[sk[0]])
            P.op("dve", lambda e, Gj=Gj, mb=mb, t3=t3: e.tensor_tensor(t3(1), Gj, mb(0), ALU.is_ge), r=[GSS[1], sk[0]], w=[sk[1]])
            P.op("dve", lambda e, Gj=Gj, t3=t3: e.scalar_tensor_tensor(t3(2), t3(1), NEG, Gj, ALU.mult, ALU.add), r=[GSS[1], sk[1]], w=[sk[2]])
            P.op("dve", lambda e, t3=t3: e.tensor_reduce(m2(3), t3(2), AX.X, ALU.max), r=[sk[2]], w=[sk[3]])
            P.op("dve", lambda e, mb=mb, t3=t3: e.tensor_tensor(t3(1), t3(2), mb(3), ALU.is_ge), r=[sk[2], sk[3]], w=[sk[1]])
            P.op("dve", lambda e, t3=t3: e.scalar_tensor_tensor(t3(4), t3(1), NEG, t3(2), ALU.mult, ALU.add), r=[sk[2], sk[1]], w=[sk[4]])
            P.op("dve", lambda e, t3=t3: e.tensor_reduce(m2(5), t3(4), AX.X, ALU.max), r=[sk[4]], w=[sk[5]])
            P.op("dve", lambda e, Gj=Gj, mb=mb, t3=t3: e.tensor_tensor(t3(1), Gj, mb(5), ALU.is_ge), r=[GSS[1], sk[5]], w=[sk[1]])
            SELj = SELa[:, qa * 64:(qa + 2) * 64].rearrange("p (g i) -> p g i", i=8)[:, :, 0:j]
            P.op("dve", lambda e, SELj=SELj, t3=t3: e.tensor_copy(SELj, t3(1)), r=[sk[1]], w=[SEL_k])
            for half in range(2):
                for q2 in range(2):
                    dstu = UNS[:, qa + q2, :, half * 32:half * 32 + 16].rearrange("p h (l i) -> p h l i", i=8)[:, :, :, 0:j]
                    srcu = STMP[1][0][:, q2 * 64:(q2 + 1) * 64].rearrange("p (h l i) -> p h l i", l=2, i=8)[:, :, :, 0:j]
                    P.op("dve", lambda e, dstu=dstu, srcu=srcu: e.tensor_scalar(dstu, srcu, -1.0, 1.0, ALU.mult, ALU.add), r=[sk[1]], w=[UNS_k])
        for hp in range(4):
            for t in range(16):
                i = t // 2
                P.op("pe", lambda e, hp=hp, t=t, i=i: e.matmul(PS[4][0:48, hp * 128:(hp + 1) * 128], CB[:, C_ONES4 + 7 - i:C_ONES4 + 55 - i],
                                                          VT[:, t, hp * 128:(hp + 1) * 128], start=(t == 0), stop=(t == 15)),
                     r=[("VT", t), "CB"], w=[("ps", 4)], c=0.08)
        nbm_b = bass.AP(NBM[:].tensor, NBM[0:48, :].offset, [list(NBM[0:48, :].ap[0]), [0, 4], [1, 128]])
        v1 = VST1[0][0:48, :].rearrange("p (h n) -> p h n", h=4)
        vh = VSHI[0][0:48, :].rearrange("p (h n) -> p h n", h=4)
        P.op("dve", lambda e, nbm_b=nbm_b, v1=v1: e.tensor_tensor(v1, PS[4][0:48, :].rearrange("p (h n) -> p h n", h=4), nbm_b, ALU.mult),
             r=[("ps", 4), "NBM"], w=[VST1[1]])
        P.op("dve", lambda e: e.tensor_copy(VSHI[0][0:48, :], VST1[0][0:48, :]), r=[VST1[1]], w=[VSHI[1]])
        P.op("dve", lambda e: e.tensor_tensor(NVS[32:48, :, :], VST1[0][32:48, :].rearrange("p (h n) -> p h n", h=4),
                                            VSHI[0][32:48, :].rearrange("p (h n) -> p h n", h=4), ALU.subtract),
             r=[VST1[1], VSHI[1]], w=["NVS"])
        P.op("dve", lambda e: e.tensor_copy(NVS[0:16, :, :], VSHI[0][0:16, :].rearrange("p (h n) -> p h n", h=4)), r=[VSHI[1]], w=["NVS"])

        sgrp = [0]
        for j in range(8):
            P.ph = 'B%d.%d' % (s, j)
            jb = j % 2
            for hp in range(4):
                qsl = QKG[:, hp, j * 256:(j + 1) * 256]
                ob = 4 + (j * 4 + hp) % 2
                SELT = SELT2[(j * 4 + hp) % 2]
                UNST = UNST2[(j * 4 + hp) % 2]
                QP, QP_k = QP2[(j * 4 + hp) % 2]
                RD = RD2[(j * 4 + hp) % 2]
                T2 = T22[(j * 4 + hp) % 2]
                if j >= 4:
                    qa = 2 * (j - 4)
                    for q2 in range(2):
                        P.op("pe", lambda e, qa=qa, q2=q2, hp=hp, ob=ob: e.transpose(psb(ob)[0:16, q2 * 128:(q2 + 1) * 128], SEL[:, qa + q2, hp, :], IDENT),
                             r=[SEL_k, "CB"], w=[("ps", ob)], c=0.08)
                        P.op("pe", lambda e, qa=qa, q2=q2, hp=hp, ob=ob: e.transpose(psb(ob)[0:48, 256 + q2 * 128:256 + (q2 + 1) * 128], UNS[:, qa + q2, hp, :], IDENT),
                             r=[UNS_k, "CB"], w=[("ps", ob)], c=0.08)
                    P.op("dve", lambda e, SELT=SELT, ob=ob: e.tensor_copy(SELT[0][0:16, :], psb(ob)[0:16, 0:256]), r=[("ps", ob)], w=[SELT[1]], c=0.2)
                    P.op("dve", lambda e, UNST=UNST, ob=ob: e.tensor_copy(UNST[0][0:48, :], psb(ob)[0:48, 256:512]), r=[("ps", ob)], w=[UNST[1]], c=0.2)
                    for i0 in range(0, j, 2):
                        ni = min(2, j - i0)
                        for ii in range(ni):
                            i = i0 + ii
                            P.op("pe", lambda e, i=i, ii=ii, SELT=SELT, ob=ob: e.matmul(PS[ob][:, ii * 256:(ii + 1) * 256], CB[0:16, C_ESEL + i * 128:C_ESEL + (i + 1) * 128],
                                                                              SELT[0][0:16, :], start=True, stop=True),
                                 r=[SELT[1], "CB"], w=[("ps", ob)], c=0.14)
                        q_b = bass.AP(qsl.tensor, qsl.offset, [list(qsl.ap[0]), [0, ni], [1, 256]])
                        P.op("dve", lambda e, i0=i0, ni=ni, q_b=q_b, QP=QP, ob=ob: e.tensor_tensor(
                            QP[:, i0:i0 + ni, :], q_b, PS[ob][:, 0:ni * 256].rearrange("p (i n) -> p i n", i=ni), ALU.mult),
                            r=[("ps", ob), qk_key(hp)], w=[QP_k], c=0.4 * ni)
                    P.op("pe", lambda e, hp=hp, ob=ob, UNST=UNST: e.matmul(PS[ob][:, 0:256], NVS[0:48, hp, :], UNST[0][0:48, :], start=True, stop=False),
                         r=["NVS", UNST[1]], w=[("ps", ob)], c=0.14)
                    P.op("pe", lambda e, ob=ob, UNST=UNST: e.matmul(PS[ob][:, 256:512], DCOR, UNST[0][0:48, :], start=False, stop=False),
                         r=["CB", UNST[1]], w=[("ps", ob)], c=0.14)
                tiles = []
                for i in range(j):
                    tiles.append((2 * i, 0, 256, i))
                    tiles.append((2 * i + 1, 0, 256, i))
                tiles.append((2 * j, 0, 256, j))
                tiles.append((2 * j + 1, 128, 256, j))
                ntile = len(tiles)
                first = [j < 4]
                for g in range(0, ntile, 2):
                    b = sgrp[0] % 2
                    sgrp[0] += 1
                    ba, bb = b * 2, b * 2 + 1
                    grp = tiles[g:g + 2]
                    for slot, (t, qlo, qhi, i) in enumerate(grp):
                        rhs_t = QP[:, i, :] if (j >= 4 and i < j) else qsl
                        rk = [QP_k] if (j >= 4 and i < j) else [qk_key(hp)]
                        dg_ = (i == j)
                        P.op("pe", lambda e, ba=ba, slot=slot, t=t, qlo=qlo, qhi=qhi, rhs_t=rhs_t, hp=hp, dg_=dg_: e.matmul(
                            PS[ba][:, slot * 256 + qlo:slot * 256 + qhi], QKG[0:64, 4 + hp, t * 128:(t + 1) * 128], rhs_t[0:64, qlo:qhi], start=True, stop=not dg_),
                            r=rk + [qk_key(4 + hp)], w=[("ps", ba)], c=0.19, md="r")
                        P.op("pe", lambda e, bb=bb, slot=slot, t=t, qlo=qlo, qhi=qhi, rhs_t=rhs_t, hp=hp, dg_=dg_: e.matmul(
                            PS[bb][:, slot * 256 + qlo:slot * 256 + qhi], QKG[64:128, 4 + hp, t * 128:(t + 1) * 128], rhs_t[64:128, qlo:qhi], start=True, stop=not dg_),
                            r=rk + [qk_key(4 + hp)], w=[("ps", bb)], c=0.02, md="r")
                        if dg_:
                            mc = slot * 256 + (0 if t % 2 == 0 else 128)
                            seq_ = [(ba, 0, 0, False), (bb, 64, 0, False), (ba, 0, 1, True), (bb, 64, 1, True)]
                            for (bk_, r0_, var_, last_) in seq_:
                                lw_ = CB[r0_:r0_ + 64, C_IDENT:C_IDENT + 128] if var_ == 0 else MASKN[r0_:r0_ + 64, 1, :]
                                rh_ = MASKN[r0_:r0_ + 64, 0, :] if var_ == 0 else MASKN[r0_:r0_ + 64, 2, :]
                                P.op("pe", lambda e, bk_=bk_, mc=mc, last_=last_, lw_=lw_, rh_=rh_: e.matmul(
                                    PS[bk_][:, mc:mc + 128], lw_, rh_, start=False, stop=last_),
                                    r=["CB"], w=[("ps", bk_), "mchain"], c=0.06, md="r")
                    diag = (grp[0][3] == j)
                    if not diag:
                        P.op("act", lambda e, b=b, ba=ba: e.activation(PTAB[b], PSALL[:, ba * 512:(ba + 2) * 512], AF.Exp),
                             r=[("ps", ba), ("ps", bb)], w=[PTA[b][1], PTBb[b][1]], c=1.05)
                    ranges = [(0, 256), (384, 512)] if diag else []
                    for (c0, c1) in ranges:
                        P.op("act", lambda e, b=b, ba=ba, c0=c0, c1=c1: e.activation(PTA[b][0][:, c0:c1], PS[ba][:, c0:c1], AF.Exp),
                             r=[("ps", ba)], w=[PTA[b][1]])
                        P.op("act", lambda e, b=b, bb=bb, c0=c0, c1=c1: e.activation(PTBb[b][0][:, c0:c1], PS[bb][:, c0:c1], AF.Exp),
                             r=[("ps", bb)], w=[PTBb[b][1]])
                    for slot, (t, qlo, qhi, i) in enumerate(grp):
                        st = first[0]
                        first[0] = False
                        last = (g + slot == ntile - 1)
                        pa = PTA[b][0][:, slot * 256 + qlo:slot * 256 + qhi]
                        pb = PTBb[b][0][:, slot * 256 + qlo:slot * 256 + qhi]
                        P.op("pe", lambda e, t=t, qlo=qlo, qhi=qhi, pa=pa, st=st, last=last, hp=hp, ob=ob: e.matmul(
                            PS[ob][0:64, qlo:qhi], VT[:, t, hp * 128:hp * 128 + 64], pa, start=st, stop=False),
                            r=[("VT", t), PTA[b][1]], w=[("ps", ob)], c=0.19, md="c")
                        P.op("pe", lambda e, t=t, qlo=qlo, qhi=qhi, pb=pb, st=st, last=last, hp=hp, ob=ob: e.matmul(
                            PS[ob][64:128, qlo:qhi], VT[:, t, hp * 128 + 64:hp * 128 + 128], pb, start=st, stop=False),
                            r=[("VT", t), PTBb[b][1]], w=[("ps", ob)], c=0.02, md="c")
                        P.op("pe", lambda e, qlo=qlo, qhi=qhi, pa=pa, last=last, ob=ob: e.matmul(
                            PS[ob][0:64, 256 + qlo:256 + qhi], ONES64, pa, start=False, stop=last), r=["CB", PTA[b][1]], w=[("ps", ob)], c=0.19, md="c")
                        P.op("pe", lambda e, qlo=qlo, qhi=qhi, pb=pb, last=last, ob=ob: e.matmul(
                            PS[ob][64:128, 256 + qlo:256 + qhi], ONES64, pb, start=False, stop=last), r=["CB", PTBb[b][1]], w=[("ps", ob)], c=0.02, md="c")
                OD = OD2[(j * 4 + hp) % 2]
                P.op("dve", lambda e, ob=ob, OD=OD: e.tensor_copy(OD[0], PS[ob][:, :]), r=[("ps", ob)], w=[OD[1]])
                P.op("dve", lambda e, OD=OD, RD=RD: e.reciprocal(RD[0], OD[0][:, 256:512]), r=[OD[1]], w=[RD[1]], c=1.8)
                P.op("dve", lambda e, hp=hp, j=j, RD=RD, T2=T2: e.tensor_tensor(T2[0], RD[0], QKG[:, 8 + hp, j * 256:(j + 1) * 256], ALU.mult),
                     r=[RD[1], qk_key(8 + hp)], w=[T2[1]], c=0.4)
                P.op("dve", lambda e, hp=hp, jb=jb, OD=OD, T2=T2: e.tensor_tensor(YA[jb][0][:, hp, :], OD[0][:, 0:256], T2[0], ALU.mult),
                     r=[OD[1], T2[1]], w=[YA[jb][1]], c=0.4)
            for tt in range(2):
                xs = tt
                tok = s * SEQ + j * 256 + tt * 128
                ltok = j * 256 + tt * 128
                P.op("sp", lambda e, xs=xs, tok=tok: e.dma_start(out=XT[:, xs, :], in_=x_d[tok:tok + 128, :]), w=[("XT", xs)], dma="xt%d" % xs)
                for nh in range(2):
                    wb = 6 + nh
                    for kt in range(8):
                        lhs = YL[:, kt, ltok:ltok + 128] if kt < 4 else YA[jb][0][:, kt - 4, tt * 128:(tt + 1) * 128]
                        rk = [("YL", kt)] if kt < 4 else [YA[jb][1]]
                        P.op("pe", lambda e, lhs=lhs, kt=kt, nh=nh, wb=wb: e.matmul(PS[wb][:, :], lhs, WO[:, kt, nh * 512:(nh + 1) * 512], start=(kt == 0), stop=(kt == 7)),
                             r=rk + [WO_k], w=[("ps", wb)])
                    P.op("dve", lambda e, xs=xs, nh=nh, wb=wb: e.tensor_tensor(XT[:, xs, nh * 512:(nh + 1) * 512], PS[wb][:, :], XT[:, xs, nh * 512:(nh + 1) * 512], ALU.add),
                         r=[("ps", wb), ("XT", xs)], w=[("XT", xs)])
                P.op("sp", lambda e, xs=xs, tok=tok: e.dma_start(out=out_d[tok:tok + 128, :], in_=XT[:, xs, :]), r=[("XT", xs)], dma="out%d" % xs)

    if debug:
        dumps = [("d_SM", SM[:], [128, 256], F32, []),
                 ("d_QKG", QKG[:].rearrange("p a b -> p (a b)"), [128, 12 * 2048], BF16, []),
                 ("d_VT", VT[:].rearrange("p a b -> p (a b)"), [128, 16 * 512], BF16, []),
                 ("d_YL", YL[:].rearrange("p a b -> p (a b)"), [128, 4 * 2048], BF16, []),
                 ("d_OV", OV[:], [128, OVW], F32, []), ("d_NVS", NVS[:].rearrange("p a b -> p (a b)"), [128, 512], BF16, []),
                 ("d_KMX", KMX[:].rearrange("p a b -> p (a b)"), [128, 64], BF16, [])]
        allk = set()
        for o_ in P.ops:
            allk.update(o_["r"]); allk.update(o_["w"])
        for (nm, ap_, shp, dt_, _) in dumps:
            dd = nc.dram_tensor(nm, shp, dt_, kind="ExternalOutput").ap()
            P.op("sp", lambda e, dd=dd, ap_=ap_: e.dma_start(out=dd[:, :], in_=ap_), r=list(allk), dma="out_" + nm)
    P.finalize()
    sems = {}
    for sk in P.semkeys:
        sems[sk] = es.enter_context(nc.semaphore("s_" + "_".join(str(x) for x in sk)))
    by_eng = {}
    for o in P.ops:
        by_eng.setdefault(o["eng"], []).append(o)

    def emit(engname, e):
        for o in by_eng.get(engname, []):
            for (sk, v) in o["waits"]:
                e.wait_ge(sems[sk], v)
            ins = o["fn"](e)
            if o["sem"] is not None:
                ins.then_inc(sems[o["sem"]], o["inc"])
        if engname == "sp":
            for k, v in P.final_dma.items():
                if k.startswith("out"):
                    e.wait_ge(sems[("dma", k)], v)

    with nc.Block() as block:
        @block.tensor
        def _(e):
            emit("pe", e)

        @block.scalar
        def _(e):
            emit("act", e)

        @block.vector
        def _(e):
            emit("dve", e)

        @block.gpsimd
        def _(e):
            emit("pool", e)

        @block.sync
        def _(e):
            emit("sp", e)
    es.close()
    return nc


def _host_inputs(x, c, w_ada, b_ada, norm_g, w_in, conv_w, conv_b, lru_wa, lru_ba, lru_wi, lru_bi,
                 lru_lambda, q_norm_g, k_norm_g, w_out):
    f = np.float32
    x = np.asarray(x, f); c = np.asarray(c, f)
    w_ada = np.ascontiguousarray(np.asarray(w_ada, f)[0]); b_ada = np.asarray(b_ada, f)[0]
    norm_g = np.asarray(norm_g, f)[0]; w_in = np.ascontiguousarray(np.asarray(w_in, f)[0])
    conv_w = np.asarray(conv_w, f)[0]; conv_b = np.asarray(conv_b, f)[0]
    lru_wa = np.asarray(lru_wa, f)[0]; lru_wi = np.asarray(lru_wi, f)[0]
    lru_ba = np.asarray(lru_ba, f)[0]; lru_bi = np.asarray(lru_bi, f)[0]; lam = np.asarray(lru_lambda, f)[0]
    gq = np.asarray(q_norm_g, f)[0]; gk = np.asarray(k_norm_g, f)[0]
    w_out = np.ascontiguousarray(np.asarray(w_out, f)[0])
    con = _consts()
    wbd = np.zeros((128, 8, 128), f)
    for ct in range(4):
        for g in range(2):
            wbd[g * 64:(g + 1) * 64, ct, g * 64:(g + 1) * 64] = lru_wa[2 * ct + g]
            wbd[g * 64:(g + 1) * 64, 4 + ct, g * 64:(g + 1) * 64] = lru_wi[2 * ct + g]
    wbd = wbd.reshape(128, 1024)
    dwc = np.zeros((128, 16, 128), f)
    for ct in range(4):
        for j in range(4):
            dwc[np.arange(128), ct * 4 + j, np.arange(128)] = conv_w[j, ct * 128:(ct + 1) * 128]
    dwc = dwc.reshape(128, 2048)
    maps = []
    for core in range(NCORES):
        par = np.zeros((128, NPAR), f)
        cc = c[core * NSEQ:(core + 1) * NSEQ]
        par[:, P_CT:P_CT + 16] = cc.reshape(NSEQ, 8, 128).transpose(2, 0, 1).reshape(128, 16)
        par[:, P_BADA:P_BADA + 24] = b_ada.reshape(24, 128).T
        par[:, P_NORMG:P_NORMG + 8] = norm_g.reshape(8, 128).T
        par[:, P_CONVW:P_CONVW + 16] = conv_w.reshape(4, 4, 128).transpose(2, 1, 0).reshape(128, 16)
        par[:, P_CONVB:P_CONVB + 4] = conv_b.reshape(4, 128).T
        par[:, P_LBA:P_LBA + 4] = lru_ba.reshape(4, 128).T
        par[:, P_LBI:P_LBI + 4] = lru_bi.reshape(4, 128).T
        par[:, P_LAM:P_LAM + 4] = lam.reshape(4, 128).T
        par[:, P_GQ] = np.tile(gq, 2)
        par[:, P_GK] = np.tile(gk, 2)
        maps.append({
            "x": np.ascontiguousarray(x[core * NSEQ:(core + 1) * NSEQ].reshape(NSEQ * SEQ, D)),
            "par": par, "con": con, "w_ada": w_ada, "w_in": w_in, "wbd": wbd, "w_out": w_out, "dwc": dwc,
        })
    return maps


def kernel(**inputs):
    maps = _host_inputs(**inputs)
    nc = build_program()
    res = run_bass_kernel_spmd(nc, maps, core_ids=list(range(NCORES)))
    outs = [np.asarray(r["out"]).reshape(NSEQ, SEQ, D) for r in res.results]
    return np.concatenate(outs, axis=0).astype(np.float32)
```
